# Optimizing a Trainium2 kernel written in Bass

```python
import math
import jax, jax.numpy as jnp
from jax import lax
import numpy as np

D_MODEL = 1024
BATCH = 8
SEQ = 8192
DEPTH = 1

D_HYENA = 768
D_LRU = 768
D_MIX = D_HYENA + D_LRU
N_HYENA_GROUPS = 12
N_LRU_HEADS = 12
LRU_HEAD_DIM = D_LRU // N_LRU_HEADS
D_IN = 4 * D_HYENA + 2 * D_LRU
HYENA_SHORT_CONV = 3
LRU_CONV = 4
FILTER_EMB_DIM = 33
FILTER_BANDS = (FILTER_EMB_DIM - 1) // 2
FILTER_HIDDEN = 64
FILTER_TARGET = 1e-2
FAST_DECAY_PCT = 0.3
SLOW_DECAY_PCT = 1.5
MIN_DECAY = math.log(FILTER_TARGET) / FAST_DECAY_PCT
MAX_DECAY = math.log(FILTER_TARGET) / SLOW_DECAY_PCT
LRU_C = 8.0
EPS = 1e-6

kernel_name = "hymba_hyena_rglru_bidir_block"


def _rmsnorm(x, g):
    xf = x.astype(jnp.float32)
    y = xf * lax.rsqrt(jnp.mean(xf * xf, axis=-1, keepdims=True) + EPS)
    return (y * g.astype(jnp.float32)).astype(x.dtype)


def _dwconv(u, w, b, pad_lo, pad_hi):
    L = u.shape[1]
    up = jnp.pad(u, ((0, 0), (pad_lo, pad_hi), (0, 0)))
    y = b
    for k in range(w.shape[0]):
        y = y + up[:, k:k + L, :] * w[k]
    return y


def _hyena_filter(L, w1, b1, f1, w2, b2, f2, w3, b3, f3, w4):
    f32 = jnp.float32
    t = jnp.linspace(0.0, 1.0, L, dtype=f32)[:, None]
    w = (2.0 * math.pi / L) * jnp.arange(L, dtype=f32)[:, None]
    f = jnp.linspace(1e-4, FILTER_BANDS - 1, FILTER_BANDS, dtype=f32)[None, :]
    z = jnp.concatenate([t, jnp.cos(w * f), -jnp.sin(w * f)], axis=-1)
    h = jnp.sin(f1.astype(f32) * (z @ w1.astype(f32) + b1.astype(f32)))
    h = jnp.sin(f2.astype(f32) * (h @ w2.astype(f32) + b2.astype(f32)))
    h = jnp.sin(f3.astype(f32) * (h @ w3.astype(f32) + b3.astype(f32)))
    h = h @ w4.astype(f32)
    deltas = jnp.linspace(MIN_DECAY, MAX_DECAY, D_HYENA, dtype=f32)
    window = jnp.exp(-t * jnp.abs(deltas)[None, :])
    h_fwd = h[:, :D_HYENA] * window
    h_bwd = h[:, D_HYENA:] * window
    k = jnp.concatenate([h_fwd, jnp.zeros((1, D_HYENA), f32), h_bwd[:0:-1]], axis=0)
    return k / (jnp.sum(jnp.abs(k), axis=0, keepdims=True) + EPS)


def _fft_conv(u, k):
    L = u.shape[1]
    U = jnp.fft.rfft(u, n=2 * L, axis=1)
    K = jnp.fft.rfft(k, n=2 * L, axis=0)
    return jnp.fft.irfft(U * K[None], n=2 * L, axis=1)[:, :L]


def _lin_comb(c1, c2):
    a1, b1 = c1
    a2, b2 = c2
    return a1 * a2, a2 * b1 + b2


def _rglru_dir(xb, wa, ba, wx, bx, lam):
    B, L, _ = xb.shape
    xh = xb.reshape(B, L, N_LRU_HEADS, LRU_HEAD_DIM)
    r = jax.nn.sigmoid(jnp.einsum('blhi,hij->blhj', xh, wa).reshape(B, L, D_LRU) + ba)
    i = jax.nn.sigmoid(jnp.einsum('blhi,hij->blhj', xh, wx).reshape(B, L, D_LRU) + bx)
    log_a = -LRU_C * r * jax.nn.softplus(-lam)
    a = jnp.exp(log_a)
    mult = jnp.sqrt(-jnp.expm1(2.0 * log_a))
    mult = mult.at[:, 0].set(1.0)
    _, h = lax.associative_scan(_lin_comb, (a, mult * (i * xb)), axis=1)
    return h


def setup_inputs(seed: int = 0) -> dict:
    key = jax.random.key(seed)
    ks = iter(jax.random.split(key, 40))
    f32 = jnp.float32

    def nrm(shape, scale):
        return jax.random.normal(next(ks), shape, f32) * scale

    x = jax.random.normal(next(ks), (BATCH, SEQ, D_MODEL), f32)
    a_c = jax.random.uniform(next(ks), (DEPTH, 2, D_LRU), f32, 0.9, 0.999)
    a_base = a_c ** (1.0 / LRU_C)
    lru_lam = jnp.log(a_base) - jnp.log1p(-a_base)
    return {
        "x": x,
        "norm_g": 1.0 + nrm((DEPTH, D_MODEL), 0.02),
        "w_in": nrm((DEPTH, D_MODEL, D_IN), D_MODEL ** -0.5),
        "hy_conv_w": nrm((DEPTH, HYENA_SHORT_CONV, 3 * D_HYENA), HYENA_SHORT_CONV ** -0.5),
        "hy_conv_b": nrm((DEPTH, 3 * D_HYENA), 0.02),
        "flt_w1": nrm((DEPTH, FILTER_EMB_DIM, FILTER_HIDDEN), FILTER_EMB_DIM ** -0.5),
        "flt_b1": nrm((DEPTH, FILTER_HIDDEN), 0.02),
        "flt_f1": 1.0 + nrm((DEPTH, FILTER_HIDDEN), 0.02),
        "flt_w2": nrm((DEPTH, FILTER_HIDDEN, FILTER_HIDDEN), FILTER_HIDDEN ** -0.5),
        "flt_b2": nrm((DEPTH, FILTER_HIDDEN), 0.02),
        "flt_f2": 1.0 + nrm((DEPTH, FILTER_HIDDEN), 0.02),
        "flt_w3": nrm((DEPTH, FILTER_HIDDEN, FILTER_HIDDEN), FILTER_HIDDEN ** -0.5),
        "flt_b3": nrm((DEPTH, FILTER_HIDDEN), 0.02),
        "flt_f3": 1.0 + nrm((DEPTH, FILTER_HIDDEN), 0.02),
        "flt_w4": nrm((DEPTH, FILTER_HIDDEN, 2 * D_HYENA), FILTER_HIDDEN ** -0.5),
        "hy_skip": nrm((DEPTH, D_HYENA), 0.5),
        "lru_conv_w": nrm((DEPTH, LRU_CONV, D_LRU), LRU_CONV ** -0.5),
        "lru_conv_b": nrm((DEPTH, D_LRU), 0.02),
        "lru_wa": nrm((DEPTH, 2, N_LRU_HEADS, LRU_HEAD_DIM, LRU_HEAD_DIM), LRU_HEAD_DIM ** -0.5),
        "lru_ba": nrm((DEPTH, 2, D_LRU), 0.02),
        "lru_wx": nrm((DEPTH, 2, N_LRU_HEADS, LRU_HEAD_DIM, LRU_HEAD_DIM), LRU_HEAD_DIM ** -0.5),
        "lru_bx": nrm((DEPTH, 2, D_LRU), 0.02),
        "lru_lam": lru_lam,
        "hy_out_g": 1.0 + nrm((DEPTH, D_HYENA), 0.02),
        "lru_out_g": 1.0 + nrm((DEPTH, D_LRU), 0.02),
        "w_out": nrm((DEPTH, D_MIX, D_MODEL), D_MIX ** -0.5),
        "final_g": 1.0 + nrm((D_MODEL,), 0.02),
    }


def reference(x, norm_g, w_in, hy_conv_w, hy_conv_b, flt_w1, flt_b1, flt_f1,
              flt_w2, flt_b2, flt_f2, flt_w3, flt_b3, flt_f3, flt_w4, hy_skip,
              lru_conv_w, lru_conv_b, lru_wa, lru_ba, lru_wx, lru_bx, lru_lam,
              hy_out_g, lru_out_g, w_out, final_g):
    f32 = jnp.float32
    L = x.shape[1]
    for l in range(DEPTH):
        xn = _rmsnorm(x, norm_g[l])
        proj = jnp.einsum('bld,de->ble', xn, w_in[l]).astype(f32)
        o = 0
        hy_in = proj[..., o:o + 3 * D_HYENA]; o += 3 * D_HYENA
        hy_gate = proj[..., o:o + D_HYENA]; o += D_HYENA
        lru_in = proj[..., o:o + D_LRU]; o += D_LRU
        lru_gate = proj[..., o:o + D_LRU]

        hy = _dwconv(hy_in, hy_conv_w[l].astype(f32), hy_conv_b[l].astype(f32), 1, 1)
        v = hy[..., :D_HYENA]
        x0 = hy[..., D_HYENA:2 * D_HYENA]
        x1 = hy[..., 2 * D_HYENA:]
        kfilt = _hyena_filter(L, flt_w1[l], flt_b1[l], flt_f1[l], flt_w2[l], flt_b2[l],
                              flt_f2[l], flt_w3[l], flt_b3[l], flt_f3[l], flt_w4[l])
        u = v * x1
        y_hy = x0 * (_fft_conv(u, kfilt) + hy_skip[l].astype(f32) * u)

        xb = _dwconv(lru_in, lru_conv_w[l].astype(f32), lru_conv_b[l].astype(f32), 1, 2)
        h_f = _rglru_dir(xb, lru_wa[l, 0].astype(f32), lru_ba[l, 0].astype(f32),
                         lru_wx[l, 0].astype(f32), lru_bx[l, 0].astype(f32),
                         lru_lam[l, 0].astype(f32))
        h_b = _rglru_dir(xb[:, ::-1], lru_wa[l, 1].astype(f32), lru_ba[l, 1].astype(f32),
                         lru_wx[l, 1].astype(f32), lru_bx[l, 1].astype(f32),
                         lru_lam[l, 1].astype(f32))[:, ::-1]
        y_lru = h_f + h_b

        y_cat = jnp.concatenate([
            _rmsnorm(y_hy, hy_out_g[l]) * jax.nn.silu(hy_gate),
            _rmsnorm(y_lru, lru_out_g[l]) * jax.nn.silu(lru_gate)], axis=-1)
        y = jnp.einsum('ble,ed->bld', y_cat.astype(x.dtype), w_out[l])
        x = x + y.astype(x.dtype)
    return _rmsnorm(x, final_g)
```

```python
import math
from contextlib import ExitStack

import numpy as np
import ml_dtypes
import concourse.bass as bass
import concourse.mybir as mybir
from concourse.bass_utils import run_bass_kernel_spmd

F32, BF16 = mybir.dt.float32, mybir.dt.bfloat16
AF = mybir.ActivationFunctionType
ALU = mybir.AluOpType

L = 8192
DM = 1024
NFFT = 16384
DH = 768
EPS = 1e-6
MIN_DECAY = math.log(1e-2) / 0.3
MAX_DECAY = math.log(1e-2) / 1.5

_cols = {}
_n = 0
for _name, _w in [("ng", 8), ("hcw", 54), ("hcb", 18), ("lcw", 24), ("lcb", 6), ("lba", 12), ("lbx", 12),
                  ("lam", 12), ("hskip", 6), ("hg", 6), ("lg", 6), ("flt", 6), ("nd", 6)]:
    _cols[_name] = _n
    _n += _w
NPRM = _n


class Prog:
    ENGS = ["pe", "act", "dve", "pool", "sp"]

    def __init__(self):
        self.ops = []
        self.lw = {}
        self.rd = {}

    def pipe_begin(self):
        self._pipe = {}
        self._cur = (0, 0)

    def at(self, tile, stage):
        self._cur = (tile, stage)

    def pipe_end(self):
        slots = self._pipe
        self._pipe = None
        for k in sorted(slots, key=lambda ts: (ts[0] + ts[1], ts[1])):
            for op in slots[k]:
                self.add(*op)

    def pipe_steps(self):
        slots = self._pipe
        self._pipe = None
        keys = sorted(slots, key=lambda ts: (ts[0] + ts[1], ts[1]))
        cur = None
        for k in keys:
            t = k[0] + k[1]
            if cur is not None and t != cur:
                yield
            cur = t
            for op in slots[k]:
                self.add(*op)
        yield

    def add(self, eng, fn, reads=(), writes=(), dma=False):
        if getattr(self, "_pipe", None) is not None:
            self._pipe.setdefault(self._cur, []).append((eng, fn, list(reads), list(writes), dma))
            return -1
        idx = len(self.ops)
        deps = set()
        for k in reads:
            if k in self.lw:
                deps.add(self.lw[k])
        for k in writes:
            if k in self.lw:
                deps.add(self.lw[k])
            for r in self.rd.get(k, ()):
                deps.add(r)
        for k in reads:
            self.rd.setdefault(k, []).append(idx)
        for k in writes:
            self.lw[k] = idx
            self.rd[k] = []
        deps.discard(idx)
        self.ops.append(dict(eng=eng, fn=fn, deps=deps, dma=dma, sig=0, dsem=None, dprev=None))
        return idx

    def barrier(self):
        last = {}
        dmas = []
        for i, op in enumerate(self.ops):
            if op["dma"]:
                dmas.append(i)
            else:
                last[op["eng"]] = i
        start = getattr(self, "_bar_from", 0)
        deps = set(last.values()) | {i for i in dmas if i >= start}
        self._bar_from = len(self.ops)
        for e in self.ENGS:
            idx = len(self.ops)
            self.ops.append(dict(eng=e, fn=None, deps=set(deps), dma=False, sig=0, dsem=None, dprev=None))
            last[e] = idx
        self.lw = {}
        self.rd = {}

    def emit(self, nc, es, npool=16):
        ops = self.ops
        sem_eng = {e: es.enter_context(nc.semaphore("s_" + e)) for e in self.ENGS}
        dma_sems = {e: [es.enter_context(nc.semaphore("d_%s%d" % (e, i))) for i in range(npool)]
                    for e in ("sp", "pool")}
        has_dep = set()
        for op in ops:
            has_dep |= op["deps"]
        cnt = {e: 0 for e in self.ENGS}
        rr = {e: 0 for e in dma_sems}
        uses = {e: [0] * npool for e in dma_sems}
        lastop = {e: [None] * npool for e in dma_sems}
        for i, op in enumerate(ops):
            e = op["eng"]
            if op["dma"]:
                s = rr[e] % npool
                rr[e] += 1
                uses[e][s] += 1
                op["dsem"] = (dma_sems[e][s], 16 * uses[e][s])
                op["dprev"] = lastop[e][s]
                lastop[e][s] = i
            elif i in has_dep and op["fn"] is not None:
                cnt[e] += 1
                op["sig"] = cnt[e]
        per_eng = {e: [i for i, op in enumerate(ops) if op["eng"] == e] for e in self.ENGS}

        def run(e, eo):
            known = {}
            for i in per_eng[e]:
                op = ops[i]
                waits = {}
                deps = set(op["deps"])
                if op["dprev"] is not None:
                    deps.add(op["dprev"])
                for d in deps:
                    D = ops[d]
                    if D["fn"] is None:
                        continue
                    if D["dma"]:
                        s, v = D["dsem"]
                    else:
                        if D["eng"] == e and e == "pe":
                            continue
                        s, v = sem_eng[D["eng"]], D["sig"]
                        assert v > 0
                    key = id(s)
                    if known.get(key, 0) >= v:
                        continue
                    if key not in waits or waits[key][1] < v:
                        waits[key] = (s, v)
                for key, (s, v) in waits.items():
                    eo.wait_ge(s, v)
                    known[key] = v
                if op["fn"] is None:
                    continue
                ins = op["fn"](eo)
                if op["dma"]:
                    ins.then_inc(op["dsem"][0], 16)
                elif op["sig"]:
                    ins.then_inc(sem_eng[e], 1)
            if e in dma_sems:
                for s in range(npool):
                    if uses[e][s]:
                        eo.wait_ge(dma_sems[e][s], 16 * uses[e][s])

        block = es.enter_context(nc.Block())

        @block.tensor
        def _(eo):
            run("pe", eo)

        @block.scalar
        def _(eo):
            run("act", eo)

        @block.vector
        def _(eo):
            run("dve", eo)

        @block.gpsimd
        def _(eo):
            run("pool", eo)

        @block.sync
        def _(eo):
            run("sp", eo)


def _bf(a):
    return np.ascontiguousarray(a.astype(np.float32)).astype(ml_dtypes.bfloat16)


_CONST = None


def _constants():
    global _CONST
    if _CONST is not None:
        return _CONST
    c = {}
    c["c_ident"] = _bf(np.eye(128))
    n = np.arange(NFFT)
    ti = np.where(n < L, n, NFFT - n)
    ti[L] = 0
    t = np.linspace(0.0, 1.0, L, dtype=np.float32)
    w = ((2.0 * math.pi / L) * np.arange(L, dtype=np.float32)).astype(np.float32)
    f = np.linspace(1e-4, 15, 16, dtype=np.float32)
    wf = (w[:, None] * f[None, :]).astype(np.float32)
    z = np.concatenate([t[:, None], np.cos(wf), -np.sin(wf)], axis=-1).astype(np.float32)
    c["c_zc"] = np.ascontiguousarray(z[ti].T)
    c["c_trow"] = np.ascontiguousarray(np.broadcast_to(t[ti][None, :], (128, NFFT))).astype(np.float32)
    n1 = np.arange(128, dtype=np.float64)[:, None]
    k1 = np.arange(65, dtype=np.float64)[None, :]
    ang = 2 * np.pi * n1 * k1 / 128
    c["c_F1"] = _bf(np.concatenate([np.cos(ang), -np.sin(ang)], axis=1))
    n2 = np.arange(128, dtype=np.float64)[:, None, None]
    kk1 = np.arange(65, dtype=np.float64)[None, :, None]
    k2 = np.arange(128, dtype=np.float64)[None, None, :]
    th = 2 * np.pi * n2 * (kk1 + 128 * k2) / NFFT
    Mr, Mi = np.cos(th), -np.sin(th)
    c["c_M"] = _bf(np.stack([Mr, Mi, -Mi], axis=2))
    thT = np.transpose(th, (2, 1, 0))
    Gr, Gi = np.cos(thT), np.sin(thT)
    c["c_G"] = _bf(np.stack([Gr, Gi, -Gi], axis=2))
    nn1 = np.arange(64, dtype=np.float64)[None, :]
    rows = []
    for k in range(65):
        ck = 1.0 if k in (0, 64) else 2.0
        rows.append(ck * np.cos(2 * np.pi * nn1 * k / 128) / NFFT)
    for k in range(1, 64):
        rows.append(-2.0 * np.sin(2 * np.pi * nn1 * k / 128) / NFFT)
    c["c_E"] = _bf(np.concatenate(rows, axis=0))
    _CONST = c
    return c


def _pack_params(inp):
    p = np.zeros((128, NPRM), np.float32)

    def put(name, arr):
        p[:, _cols[name]:_cols[name] + arr.shape[1]] = arr

    put("ng", inp["norm_g"][0].reshape(8, 128).T)
    hcw = inp["hy_conv_w"][0]
    put("hcw", hcw.reshape(3, 18, 128).transpose(2, 0, 1).reshape(128, 54))
    put("hcb", inp["hy_conv_b"][0].reshape(18, 128).T)
    put("lcw", inp["lru_conv_w"][0].reshape(4, 6, 128).transpose(2, 0, 1).reshape(128, 24))
    put("lcb", inp["lru_conv_b"][0].reshape(6, 128).T)
    put("lba", inp["lru_ba"][0].reshape(2, 6, 128).transpose(2, 0, 1).reshape(128, 12))
    put("lbx", inp["lru_bx"][0].reshape(2, 6, 128).transpose(2, 0, 1).reshape(128, 12))
    put("lam", inp["lru_lam"][0].reshape(2, 6, 128).transpose(2, 0, 1).reshape(128, 12))
    put("hskip", inp["hy_skip"][0].reshape(6, 128).T)
    put("hg", inp["hy_out_g"][0].reshape(6, 128).T)
    put("lg", inp["lru_out_g"][0].reshape(6, 128).T)
    flt = np.zeros((128, 6), np.float32)
    for li, (fk, bk) in enumerate([("flt_f1", "flt_b1"), ("flt_f2", "flt_b2"), ("flt_f3", "flt_b3")]):
        flt[:64, 2 * li] = inp[fk][0]
        flt[:64, 2 * li + 1] = inp[bk][0]
    put("flt", flt)
    deltas = np.linspace(MIN_DECAY, MAX_DECAY, DH, dtype=np.float32)
    put("nd", (-np.abs(deltas)).reshape(6, 128).T)
    return p


def _blockdiag(inp):
    out = np.zeros((2, 2, 6, 128, 128), np.float32)
    for d in range(2):
        for gi, key in enumerate(["lru_wa", "lru_wx"]):
            wgt = inp[key][0, d]
            for ch in range(6):
                out[d, gi, ch, 0:64, 0:64] = wgt[2 * ch]
                out[d, gi, ch, 64:128, 64:128] = wgt[2 * ch + 1]
    return out


def build(debug=False, phases=(0, 1, 2, 3, 4, 5), lim=99, nch=6):
    nc = bass.Bass("TRN2", target_bir_lowering=False)
    P = Prog()

    def din(name, shape, dt=F32):
        return nc.dram_tensor(name, list(shape), dt, kind="ExternalInput").ap()

    def dscr(name, shape, dt):
        return nc.dram_tensor(name, list(shape), dt, kind=("ExternalOutput" if debug else "Internal")).ap()

    x = din("x", [L, DM])
    w_in = din("w_in", [DM, 4608])
    w_out = din("w_out", [1536, DM])
    prm_d = din("prm", [128, NPRM])
    bd_d = din("bd", [2, 2, 6, 128, 128])
    fg_d = din("fg", [128, DM])
    w1_d = din("flt_w1", [33, 64])
    w2_d = din("flt_w2", [64, 64])
    w3_d = din("flt_w3", [64, 64])
    w4_d = din("flt_w4", [64, 1536])
    c_ident = din("c_ident", [128, 128], BF16)
    c_zc = din("c_zc", [33, NFFT])
    c_trow = din("c_trow", [128, NFFT])
    c_F1 = din("c_F1", [128, 130], BF16)
    c_M = din("c_M", [128, 65, 3, 128], BF16)
    c_G = din("c_G", [128, 65, 3, 128], BF16)
    c_E = din("c_E", [128, 64], BF16)
    out_d = nc.dram_tensor("out", [L, DM], F32, kind="ExternalOutput").ap()

    xnTd = dscr("xnTd", [8, 128, L], BF16)
    kd = dscr("kd", [DH, NFFT], BF16)
    Kd = dscr("Kd", [6, 128, 65, 2, 128], F32)
    ud = dscr("ud", [DH, L], BF16)
    x0d = dscr("x0d", [DH, L], BF16)
    sgd = dscr("sgd", [DH, L], BF16)
    yd = dscr("yd", [DH, L], F32)
    zd = dscr("zd", [1536, L], BF16)
    ssd = dscr("ssd", [2, 128, 64], F32)

    es = ExitStack()
    with es:
        def sb(name, shape, dt):
            return es.enter_context(nc.sbuf_tensor(name, list(shape), dt))

        def ACT(out, in_, func, R, W, bias=None, scale=None, accum=None):
            kw = {}
            if bias is not None:
                kw["bias"] = bias
            if scale is not None:
                kw["scale"] = scale
            if accum is not None:
                kw["accum_out"] = accum
            P.add("act", lambda e: e.activation(out=out, in_=in_, func=func, **kw), R, W)

        def TS(out, in0, s1, s2, op0, op1, R, W, eng="dve", accum=None):
            kw = {}
            if accum is not None:
                kw["accum_out"] = accum
            if op1 is None:
                P.add(eng, lambda e: e.tensor_scalar(out=out, in0=in0, scalar1=s1, scalar2=None, op0=op0, **kw), R, W)
            else:
                P.add(eng, lambda e: e.tensor_scalar(out=out, in0=in0, scalar1=s1, scalar2=s2, op0=op0, op1=op1, **kw), R, W)

        def TT(out, in0, in1, op, R, W, eng="dve"):
            P.add(eng, lambda e: e.tensor_tensor(out=out, in0=in0, in1=in1, op=op), R, W)

        def STT(out, in0, scalar, in1, op0, op1, R, W):
            P.add("dve", lambda e: e.scalar_tensor_tensor(out=out, in0=in0, scalar=scalar, in1=in1, op0=op0, op1=op1), R, W)

        def CP(out, in_, R, W, eng="dve"):
            P.add(eng, lambda e: e.tensor_copy(out=out, in_=in_), R, W)

        def ACT_COPY(out, in_, R, W):
            ACT(out, in_, AF.Copy, R, W)

        def MM(out, pairs, R, W):
            pairs = list(pairs)

            def f(e):
                ins = None
                for i, (l, r) in enumerate(pairs):
                    ins = e.matmul(out, l, r, start=(i == 0), stop=(i == len(pairs) - 1))
                return ins
            P.add("pe", f, R, W)

        def MMS(outs_pairs, R, W):
            groups = [(o, list(p)) for o, p in outs_pairs]

            def f(e):
                ins = None
                for o, pairs in groups:
                    for i, (l, r) in enumerate(pairs):
                        ins = e.matmul(o, l, r, start=(i == 0), stop=(i == len(pairs) - 1))
                return ins
            P.add("pe", f, R, W)

        def DMA(q, out, in_, R, W):
            P.add(q, lambda e: e.dma_start(out=out, in_=in_), R, W, dma=True)

        def MSET(ap, val, R, W, eng="dve"):
            P.add(eng, lambda e: e.memset(ap, val), R, W)

        ident = sb("ident", [128, 128], BF16)
        ones = sb("ones", [128, 2], BF16)
        prm = sb("prm_s", [128, NPRM], F32)
        drv = sb("drv", [128, 64], F32)
        invn = sb("invn", [128, 8], F32)
        ss = sb("ss", [128, 2, 64], F32)
        psF = [es.enter_context(nc.psum_tensor("psF%d" % i, [128, 512], F32)) for i in range(6)]
        psB = [es.enter_context(nc.psum_tensor("psB%d" % i, [128, 1024], BF16)) for i in range(2)]

        def pc(name, j=0, w=1):
            c0 = _cols[name] + j
            return prm[:, c0:c0 + w]

        DMA("sp", ident[:], c_ident[:, :], [], ["ident"])
        DMA("sp", prm[:], prm_d[:, :], [], ["prm"])
        MSET(ones[:], 1.0, [], ["ones"])
        ACT(drv[:, 0:12], pc("lam", 0, 12), AF.Exp, ["prm"], ["drv"], scale=-1.0)
        ACT(drv[:, 0:12], drv[:, 0:12], AF.Ln, ["drv"], ["drv"], bias=1.0)
        TS(drv[:, 12:24], drv[:, 0:12], -16.0, None, ALU.mult, None, ["drv"], ["drv"])
        TS(drv[:, 0:12], drv[:, 0:12], -8.0, None, ALU.mult, None, ["drv"], ["drv"])
        for li in range(3):
            TT(drv[:, 25 + 2 * li:26 + 2 * li], pc("flt", 2 * li), pc("flt", 2 * li + 1), ALU.mult, ["prm", "drv"], ["drv"])
            TS(drv[:, 25 + 2 * li:26 + 2 * li], drv[:, 25 + 2 * li:26 + 2 * li], 1.0 / 3.0, None, ALU.mult, None, ["drv"], ["drv"])
            TS(drv[:, 24 + 2 * li:25 + 2 * li], pc("flt", 2 * li), 1.0 / 3.0, None, ALU.mult, None, ["prm", "drv"], ["drv"])

        def wprep(wst, wb, blocks, tag):
            for q, c0 in enumerate(blocks):
                DMA("sp", wst[:, :, q * 128:(q + 1) * 128],
                    w_in[:, c0:c0 + 128].rearrange("(dc p) c -> p dc c", p=128), [], [(tag, "wst", q)])
                for dc in range(8):
                    TS(wb[:, dc, q * 128:(q + 1) * 128], wst[:, dc, q * 128:(q + 1) * 128], pc("ng", dc), None,
                       ALU.mult, None, [(tag, "wst", q), "prm"], [(tag, "wb")])

        def ph0ab():
            with ExitStack() as ph:
                def sbp(name, shape, dt):
                    return ph.enter_context(nc.sbuf_tensor(name, list(shape), dt))
                w1 = sbp("w1", [33, 64], F32)
                w2 = sbp("w2", [64, 64], F32)
                w3 = sbp("w3", [64, 64], F32)
                w4 = sbp("w4", [64, 1536], F32)
                h3T = sbp("h3T", [64, NFFT], F32)
                trow = sbp("trow", [128, NFFT], F32)
                zt = [sbp("zt%d" % i, [33, 512], F32) for i in range(2)]
                hs = [sbp("hs%d" % i, [64, 512], F32) for i in range(3)]
                hq = [sbp("hq%d" % i, [64, 512], F32) for i in range(3)]
                hh = [sbp("hh%d" % i, [64, 512], F32) for i in range(3)]
                DMA("sp", w1[:], w1_d[:, :], [], ["w1"])
                DMA("sp", w2[:], w2_d[:, :], [], ["w2"])
                DMA("sp", w3[:], w3_d[:, :], [], ["w3"])
                DMA("sp", w4[:], w4_d[:, :], [], ["w4"])
                DMA("sp", trow[:], c_trow[:, :], [], ["trow"])
                P.pipe_begin()
                for j in range(32):
                    b = j % 2
                    b3 = j % 3
                    P.at(j, 0)
                    DMA("sp", zt[b][:], c_zc[:, j * 512:(j + 1) * 512], [], [("zt", b)])
                    cur = zt[b][0:33, :]
                    curk = ("zt", b)
                    for li, (wl, kdim) in enumerate([(w1, 33), (w2, 64), (w3, 64)]):
                        P.at(j, li)
                        pst = psF[(3 * j + li) % 6]
                        pk = ("psF", (3 * j + li) % 6)
                        MM(pst[0:64, :], [(wl[0:kdim, 0:64], cur)], [curk, "w%d" % (li + 1)], [pk])
                        ACT(hs[b3][:], pst[0:64, :], AF.Sin, [pk, "drv"], [("hs", b3)],
                            bias=drv[0:64, 25 + 2 * li:26 + 2 * li], scale=drv[0:64, 24 + 2 * li:25 + 2 * li])
                        TT(hq[b3][:], hs[b3][:], hs[b3][:], ALU.mult, [("hs", b3)], [("hq", b3)])
                        TS(hq[b3][:], hq[b3][:], -4.0, 3.0, ALU.mult, ALU.add, [("hq", b3)], [("hq", b3)])
                        if li < 2:
                            TT(hh[b3][:], hq[b3][:], hs[b3][:], ALU.mult, [("hq", b3), ("hs", b3)], [("hh", b3)])
                            cur = hh[b3][:]
                            curk = ("hh", b3)
                        else:
                            TT(h3T[:, j * 512:(j + 1) * 512], hq[b3][:], hs[b3][:], ALU.mult,
                               [("hq", b3), ("hs", b3)], [("h3T", j)])
                P.pipe_end()
                wt = [sbp("wt%d" % i, [128, 512], F32) for i in range(2)]
                kt = [sbp("kt%d" % i, [128, 512], F32) for i in range(2)]
                kj = [sbp("kj%d" % i, [128, 512], F32) for i in range(2)]
                kb = [sbp("kb%d" % i, [128, 512], BF16) for i in range(2)]
                ksum = sbp("ksum", [128, 32], F32)
                kred = sbp("kred", [128, 1], F32)
                for ch in range(6):
                    P.pipe_begin()
                    for j in range(32):
                        b = j % 2
                        col0 = ch * 128 + (0 if j < 16 else DH)
                        pst = psF[j % 6]
                        pk = ("psF", j % 6)
                        P.at(j, 0)
                        MM(pst[:, :], [(w4[0:64, col0:col0 + 128], h3T[0:64, j * 512:(j + 1) * 512])], [("h3T", j), "w4"], [pk])
                        ACT(wt[b][:], trow[:, j * 512:(j + 1) * 512], AF.Exp, ["trow", "prm"], [("wt", b)], scale=pc("nd", ch))
                        P.at(j, 1)
                        TT(kt[b][:], pst[:, :], wt[b][:], ALU.mult, [pk, ("wt", b)], [("kt", b)])
                        if j == 16:
                            MSET(kt[b][:, 0:1], 0.0, [], [("kt", b)])
                        P.at(j, 2)
                        ACT(kj[b][:], kt[b][:], AF.Abs, [("kt", b)], [("kj", b), ("ksum", j)], accum=ksum[:, j:j + 1])
                        CP(kb[b][:], kt[b][:], [("kt", b)], [("kb", b)])
                        P.at(j, 3)
                        DMA("pool", kd[ch * 128:(ch + 1) * 128, j * 512:(j + 1) * 512], kb[b][:], [("kb", b)], [("kd", ch, j)])
                    P.pipe_end()
                    P.add("dve", lambda e: e.tensor_reduce(out=kred[:], in_=ksum[:], axis=mybir.AxisListType.X, op=ALU.add),
                          [("ksum", j) for j in range(32)], ["kred"])
                    TS(kred[:], kred[:], EPS, None, ALU.add, None, ["kred"], ["kred"])
                    P.add("dve", lambda e, ch=ch: e.reciprocal(out=invn[:, ch:ch + 1], in_=kred[:]), ["kred"], ["invn"])
            P.barrier()

        def ph0c(Mres, F1):
            with ExitStack() as ph:
                def sbp(name, shape, dt):
                    return ph.enter_context(nc.sbuf_tensor(name, list(shape), dt))
                kT = sbp("kT", [128, 128, 128], BF16)
                Aa = sbp("Aa", [128, 130, 128], BF16)
                kg = [sbp("kg%d" % i, [128, 6, 2, 128], F32) for i in range(2)]
                for ch in range(6):
                    for q in range(8):
                        DMA("sp", kT[:, q * 16:(q + 1) * 16, :],
                            kd[ch * 128 + q * 16:ch * 128 + (q + 1) * 16, :].rearrange("c (a b) -> a c b", b=128),
                            [("kd", ch, j) for j in range(32)], [("kT", q)])
                    fft_s1(MM, ACT, CP, psF, lambda c: kT[0:128, c, :], F1, Aa, 128, lambda c: ("kT", c // 16))
                    for g in range(11):
                        b = g % 2
                        nk = min(6, 65 - 6 * g)
                        for kk in range(nk):
                            k1 = g * 6 + kk
                            pst = psF[k1 % 6]
                            pk = ("psF", k1 % 6)
                            fft_s3(MMS, pst[:, 0:256], Mres, k1, Aa, k1, ["M"] + AA_KEYS, [pk])
                            ev = ACT_COPY if k1 % 2 == 0 else CP
                            ev(kg[b][:, kk, :, :], pst[:, 0:256].rearrange("p (r c) -> p r c", r=2), [pk], [("kg", b)])
                        DMA("pool", Kd[ch, :, g * 6:g * 6 + nk, :, :], kg[b][:, 0:nk, :, :], [("kg", b)], [("Kd", ch, g)])
            P.barrier()

        def ph1():
            with ExitStack() as ph:
                def sbp(name, shape, dt):
                    return ph.enter_context(nc.sbuf_tensor(name, list(shape), dt))
                NX = 6
                xin = [sbp("xin%d" % i, [128, DM], F32) for i in range(NX)]
                xjk = sbp("xjk", [128, DM], F32)
                xs = [sbp("xs%d" % i, [128, DM], BF16) for i in range(2)]
                xT = [sbp("xT%d" % i, [128, 8, 512], BF16) for i in range(2)]
                s1 = sbp("s1", [128, 64], F32)
                r1 = sbp("r1", [128, 64], F32)
                P.pipe_begin()
                for g in range(16):
                    gb = g % 2
                    for s in range(4):
                        tt = g * 4 + s
                        b = tt % 2
                        bx = tt % NX
                        P.at(tt, 0)
                        DMA("sp", xin[bx][:], x[tt * 128:(tt + 1) * 128, :], [], [("xin", bx)])
                        P.at(tt, 1)
                        ACT(xjk[:], xin[bx][:], AF.Square, [("xin", bx)], ["xjk", ("s1", tt)], accum=s1[:, tt:tt + 1])
                        P.at(tt, 2)
                        TS(r1[:, tt:tt + 1], s1[:, tt:tt + 1], 1.0 / DM, EPS, ALU.mult, ALU.add, [("s1", tt)], [("r1", tt)])
                        P.at(tt, 3)
                        ACT(r1[:, tt:tt + 1], r1[:, tt:tt + 1], AF.Sqrt, [("r1", tt)], [("r1", tt)])
                        P.at(tt, 4)
                        P.add("dve", lambda e, tt=tt: e.reciprocal(out=r1[:, tt:tt + 1], in_=r1[:, tt:tt + 1]), [("r1", tt)], [("r1", tt)])
                        P.at(tt, 5)
                        TS(xs[b][:], xin[bx][:], r1[:, tt:tt + 1], None, ALU.mult, None, [("xin", bx), ("r1", tt)], [("xs", b)])
                        P.at(tt, 6)

                        def trs(e, b=b):
                            ins = None
                            for dc in range(8):
                                ins = e.transpose(psB[b][:, dc * 128:(dc + 1) * 128], xs[b][:, dc * 128:(dc + 1) * 128], ident[:])
                            return ins
                        P.add("pe", trs, [("xs", b), "ident"], [("psB", b)])
                        P.at(tt, 7)
                        CP(xT[gb][:, :, s * 128:(s + 1) * 128], psB[b][:, :].rearrange("p (dc t) -> p dc t", dc=8),
                           [("psB", b)], [("xT", gb)])
                        if s == 3:
                            P.at(tt, 8)
                            DMA("pool", xnTd[:, :, g * 512:(g + 1) * 512].rearrange("dc p t -> p dc t"), xT[gb][:],
                                [("xT", gb)], [("xnTd", g)])
                P.pipe_end()
            P.barrier()

        def ph2():
            with ExitStack() as ph:
                def sbp(name, shape, dt):
                    return ph.enter_context(nc.sbuf_tensor(name, list(shape), dt))
                wst = sbp("wst", [128, 8, 256], F32)
                wb = sbp("wb", [128, 8, 256], BF16)
                xt = [sbp("xt%d" % i, [128, 8, 512], BF16) for i in range(2)]
                raw = sbp("raw", [128, L + 8], BF16)
                xb2 = [sbp("xb%d" % i, [128, L], BF16) for i in range(2)]
                sg2 = [sbp("sg%d" % i, [128, L], BF16) for i in range(2)]
                hf = sbp("hf", [128, L], BF16)
                ra = sbp("ra", [128, L], F32)
                ix = sbp("ix", [128, L], BF16)
                dg = sbp("dg", [128, 4, 128], BF16)
                bdf = sbp("bdf", [128, 4, 128], F32)
                bdb2 = [sbp("bdb%d" % i, [128, 4, 128], BF16) for i in range(2)]
                ti_ = [sbp("ti%d" % i, [128, 512], BF16) for i in range(2)]
                a2_ = [sbp("a2%d" % i, [128, 512], F32) for i in range(2)]
                tm_ = [sbp("tm%d" % i, [128, 512], BF16) for i in range(2)]
                tb_ = [sbp("tb%d" % i, [128, 512], BF16) for i in range(2)]
                hb_ = [sbp("hb%d" % i, [128, 512], F32) for i in range(2)]
                ty_ = [sbp("ty%d" % i, [128, 512], F32) for i in range(2)]
                tq_ = [sbp("tq%d" % i, [128, 512], BF16) for i in range(2)]
                tz_ = [sbp("tz%d" % i, [128, 512], BF16) for i in range(2)]
                tg2_ = [sbp("tg2%d" % i, [128, 512], BF16) for i in range(2)]
                MSET(raw[:, 0:1], 0.0, [], ["rawpad"])
                MSET(raw[:, L + 1:L + 8], 0.0, [], ["rawpad"])

                def X(ch):
                    cp = ch % 2
                    xb, sg, bdb = xb2[cp], sg2[cp], bdb2[cp]
                    wprep(wst, wb, [3072 + ch * 128, 3840 + ch * 128], "l")
                    for k in range(4):
                        TS(dg[:, k, :], ident[:], pc("lcw", k * 6 + ch), None, ALU.mult, None, ["ident", "prm"], ["dg"])
                    for d in range(2):
                        for gi in range(2):
                            DMA("sp", bdf[:, d * 2 + gi, :], bd_d[d, gi, ch, :, :], [], [("bdf", d * 2 + gi)])
                            CP(bdb[:, d * 2 + gi, :], bdf[:, d * 2 + gi, :], [("bdf", d * 2 + gi)], [("bdb", cp)])
                    yield

                    def conv(j):
                        pcv = psF[2]
                        kc = ("psF", 2)
                        rds = [("raw", j), "dg", "rawpad"] + ([("raw", j + 1)] if j < 15 else [])
                        MM(pcv[:, :], [(dg[:, k, :], raw[:, j * 512 + k:j * 512 + k + 512]) for k in range(4)], rds, [kc])
                        ACT(xb[:, j * 512:(j + 1) * 512], pcv[:, :], AF.Identity, [kc, "prm"], [("xb", cp, j)], bias=pc("lcb", ch))
                    for j in range(16):
                        b = j % 2
                        DMA("sp", xt[b][:], xnTd[:, :, j * 512:(j + 1) * 512].rearrange("dc p t -> p dc t"),
                            [], [("xt", b)])
                        i0, i1 = 0, 1
                        p0, p1 = psF[i0], psF[i1]
                        k0, k1_ = ("psF", i0), ("psF", i1)
                        MM(p0[:, :], [(wb[:, dc, 0:128], xt[b][:, dc, :]) for dc in range(8)], [("l", "wb"), ("xt", b)], [k0])
                        MM(p1[:, :], [(wb[:, dc, 128:256], xt[b][:, dc, :]) for dc in range(8)], [("l", "wb"), ("xt", b)], [k1_])
                        CP(raw[:, 1 + j * 512:1 + (j + 1) * 512], p0[:, :], [k0], [("raw", j)])
                        ACT(sg[:, j * 512:(j + 1) * 512], p1[:, :], AF.Copy, [k1_], [("sg", cp, j)])
                        if j > 0:
                            conv(j - 1)
                        yield
                    conv(15)
                    yield

                def Y(ch):
                    cp = ch % 2
                    xb, sg, bdb = xb2[cp], sg2[cp], bdb2[cp]
                    for d in range(2):
                        order = list(range(16)) if d == 0 else list(range(15, -1, -1))
                        for j in order:
                            b = j % 2
                            sl = slice(j * 512, (j + 1) * 512)
                            p0, p1 = psF[4], psF[5]
                            k0, k1_ = ("psF", 4), ("psF", 5)
                            MM(p0[:, :], [(bdb[:, d * 2, :], xb[:, sl])], [("bdb", cp), ("xb", cp, j)], [k0])
                            MM(p1[:, :], [(bdb[:, d * 2 + 1, :], xb[:, sl])], [("bdb", cp), ("xb", cp, j)], [k1_])
                            ACT(ra[:, sl], p0[:, :], AF.Sigmoid, [k0, "prm"], [("ra", j)], bias=pc("lba", d * 6 + ch))
                            ACT(ti_[b][:], p1[:, :], AF.Sigmoid, [k1_, "prm"], [("ti", b)], bias=pc("lbx", d * 6 + ch))
                            TT(ix[:, sl], ti_[b][:], xb[:, sl], ALU.mult, [("ti", b), ("xb", cp, j)], [("ix", j)])
                            if d == 0:
                                ACT(tg2_[b][:], sg[:, sl], AF.Sigmoid, [("sg", cp, j)], [("tg2", b)])
                                TT(sg[:, sl], sg[:, sl], tg2_[b][:], ALU.mult, [("sg", cp, j), ("tg2", b)], [("sg", cp, j)], eng="pool")
                            yield "A"
                        qs = range(4) if d == 0 else range(3, -1, -1)
                        for q in qs:
                            ks = [("ra", 4 * q + i) for i in range(4)]
                            ACT(ra[:, q * 2048:(q + 1) * 2048], ra[:, q * 2048:(q + 1) * 2048], AF.Exp, ks + ["drv"], ks,
                                scale=drv[:, d * 6 + ch:d * 6 + ch + 1])
                            yield "B"
                        P.pipe_begin()
                        for n_, j in enumerate(order):
                            b = j % 2
                            sl = slice(j * 512, (j + 1) * 512)
                            P.at(n_, 0)
                            ACT(a2_[b][:], ra[:, sl], AF.Square, [("ra", j)], [("a2", b)])
                            ACT(tm_[b][:], a2_[b][:], AF.Sqrt, [("a2", b)], [("tm", b)], bias=1.0, scale=-1.0)
                            P.at(n_, 1)
                            if d == 0 and j == 0:
                                MSET(tm_[b][:, 0:1], 1.0, [], [("tm", b)], eng="pool")
                            if d == 1 and j == 15:
                                MSET(tm_[b][:, 511:512], 1.0, [], [("tm", b)], eng="pool")
                            TT(tb_[b][:], ix[:, sl], tm_[b][:], ALU.mult, [("ix", j), ("tm", b)], [("tb", b)], eng="pool")
                            P.at(n_, 2)
                            if d == 0:
                                init = 0.0 if j == 0 else hf[:, j * 512 - 1:j * 512]
                                rd = [("ra", j), ("tb", b)] + ([("hf", j - 1)] if j > 0 else [])
                                P.add("dve", lambda e, b=b, sl=sl, init=init: e.tensor_tensor_scan(
                                    out=hf[:, sl], data0=ra[:, sl], data1=tb_[b][:], initial=init, op0=ALU.mult, op1=ALU.add),
                                    rd, [("hf", j)])
                            else:
                                init = 0.0 if j == 15 else hb_[1 - b][:, 0:1]
                                rd = [("ra", j), ("tb", b)] + ([("hb", 1 - b)] if j < 15 else [])
                                P.add("dve", lambda e, b=b, sl=sl, init=init: e.tensor_tensor_scan(
                                    out=hb_[b][:, ::-1], data0=ra[:, sl][:, ::-1], data1=tb_[b][:, ::-1], initial=init,
                                    op0=ALU.mult, op1=ALU.add), rd, [("hb", b)])
                                P.at(n_, 3)
                                TT(ty_[b][:], hf[:, sl], hb_[b][:], ALU.add, [("hf", j), ("hb", b)], [("ty", b)], eng="pool")
                                ACT(tq_[b][:], ty_[b][:], AF.Square, [("ty", b)], [("tq", b)])
                                pss = psF[3]
                                c0 = (n_ % 8) * 4
                                ks = ("psF", 3)
                                MMS([(pss[:, c0 + s:c0 + s + 1], [(tq_[b][:, s * 128:(s + 1) * 128], ones[:, 0:1])]) for s in range(4)],
                                    [("tq", b), "ones"], [ks])
                                P.at(n_, 4)
                                STT(tz_[b][:], ty_[b][:], pc("lg", ch), sg[:, sl], ALU.mult, ALU.mult,
                                    [("ty", b), ("sg", cp, j), "prm"], [("tz", b)])
                                P.at(n_, 5)
                                if ch == 0:
                                    CP(ss[:, 1, j * 4:(j + 1) * 4], pss[:, c0:c0 + 4], [ks], [("ss1", j)])
                                else:
                                    TT(ss[:, 1, j * 4:(j + 1) * 4], ss[:, 1, j * 4:(j + 1) * 4], pss[:, c0:c0 + 4], ALU.add,
                                       [ks, ("ss1", j)], [("ss1", j)])
                                DMA("sp", zd[DH + ch * 128:DH + (ch + 1) * 128, sl], tz_[b][:], [("tz", b)], [("zd", 6 + ch, j)])
                        for _ in P.pipe_steps():
                            yield "C"

                def run_merged(gy, gx):
                    cnt = 0
                    cntb = 0
                    for tag in gy:
                        if tag == "C":
                            cnt += 1
                            if cnt % 2 == 0:
                                next(gx, None)
                        elif tag == "B":
                            cntb += 1
                            next(gx, None)
                            next(gx, None)
                    for _ in gx:
                        pass

                for _ in X(0):
                    pass
                for ch in range(nch):
                    if ch + 1 < nch:
                        run_merged(Y(ch), X(ch + 1))
                    else:
                        for _ in Y(ch):
                            pass
            P.barrier()

        def ph3():
            with ExitStack() as ph:
                def sbp(name, shape, dt):
                    return ph.enter_context(nc.sbuf_tensor(name, list(shape), dt))
                wst = sbp("wst3", [128, 8, 512], F32)
                wb = sbp("wb3", [128, 8, 512], BF16)
                xt = [sbp("xt3%d" % i, [128, 8, 512], BF16) for i in range(2)]
                raw3 = [sbp("raw3%d" % i, [128, L + 8], BF16) for i in range(3)]
                dg = sbp("dg3", [128, 9, 128], BF16)
                tv_ = [sbp("tv%d" % i, [128, 512], F32) for i in range(2)]
                tc_ = [[sbp("tc%d_%d" % (q, i), [128, 512], F32) for i in range(2)] for q in range(3)]
                tu_ = [sbp("tu%d" % i, [128, 512], BF16) for i in range(2)]
                tx_ = [sbp("tx%d" % i, [128, 512], BF16) for i in range(2)]
                tg_ = [sbp("tg%d" % i, [128, 512], BF16) for i in range(2)]
                for q in range(3):
                    MSET(raw3[q][:, 0:1], 0.0, [], ["rawpad3"])
                    MSET(raw3[q][:, L + 1:L + 8], 0.0, [], ["rawpad3"])
                for ch in range(6):
                    wprep(wst, wb, [ch * 128, DH + ch * 128, 2 * DH + ch * 128, 3 * DH + ch * 128], "h")
                    for j in range(16):
                        b = j % 2
                        sl = slice(j * 512, (j + 1) * 512)
                        DMA("sp", xt[b][:], xnTd[:, :, sl].rearrange("dc p t -> p dc t"), [("xnTd", j)], [("xt3", b)])
                        for q in range(4):
                            pq = psF[q]
                            MM(pq[:, :], [(wb[:, dc, q * 128:(q + 1) * 128], xt[b][:, dc, :]) for dc in range(8)],
                               [("h", "wb"), ("xt3", b)], [("psF", q)])
                            if q < 3:
                                ACT(raw3[q][:, 1 + j * 512:1 + (j + 1) * 512], pq[:, :], AF.Copy, [("psF", q)], [("raw3", q, j)])
                            else:
                                ACT(tg_[b][:], pq[:, :], AF.Silu, [("psF", q)], [("tg", b)])
                                DMA("pool", sgd[ch * 128:(ch + 1) * 128, sl], tg_[b][:], [("tg", b)], [("sgd", ch, j)])
                        for jj in ([j - 1] if j > 0 else []) + ([15] if j == 15 else []):
                            conv3(ACT, STT, TT, DMA, raw3, tc_, tv_, tu_, tx_, pc, ud, x0d, ch, jj)
            P.barrier()

        def ph4a(Mres, F1):
            with ExitStack() as ph:
                def sbp(name, shape, dt):
                    return ph.enter_context(nc.sbuf_tensor(name, list(shape), dt))
                Gres = sbp("Gres", [128, 65, 3, 128], BF16)
                Em = sbp("Em", [128, 64], BF16)
                uTp = [sbp("uTp%d" % i, [64, 8, 128], BF16) for i in range(2)]
                Aa = sbp("Aa4", [128, 130, 128], BF16)
                Bp = sbp("Bp", [128, 128, 128], BF16)
                kg = [sbp("kg4%d" % i, [128, 6, 2, 128], F32) for i in range(2)]
                tq = [sbp("tq4%d" % i, [128, 4, 2, 128], BF16) for i in range(2)]
                yT = [sbp("yT%d" % i, [64, 8, 128], F32) for i in range(3)]
                for h_ in range(5):
                    DMA("sp", Gres[:, h_ * 13:(h_ + 1) * 13, :, :], c_G[:, h_ * 13:(h_ + 1) * 13, :, :], [], ["G"])
                DMA("sp", Em[:], c_E[:, :], [], ["Em"])
                BT = Aa
                NP2 = 33
                for ch in range(6):
                    def dT(c):
                        return uTp[(c // 8) % 2][0:64, c % 8, :]
                    def pre(c, ch=ch):
                        if c % 8 == 0:
                            q = c // 8
                            DMA("sp", uTp[q % 2][:],
                                ud[ch * 128 + q * 8:ch * 128 + (q + 1) * 8, :].rearrange("c (a b) -> a c b", b=128),
                                [], [("uT", q % 2)])
                    fft_s1(MM, ACT, CP, psF, dT, F1, Aa, 64, lambda c: ("uT", (c // 8) % 2), pre=pre)

                    def s3(p):
                        pX = psF[p % 4]
                        nj = 2 if p < 32 else 1
                        for j in range(nj):
                            k1 = 2 * p + j
                            fft_s3(MMS, pX[:, j * 256:(j + 1) * 256], Mres, k1, Aa, k1, ["M"] + AA_KEYS, [("psF", p % 4)])
                    s3(0)
                    s3(1)
                    s3(2)
                    for p in range(NP2):
                        nj = 2 if p < 32 else 1
                        k1a = 2 * p
                        g = k1a // 6
                        kk = k1a % 6
                        b = g % 2
                        if kk == 0:
                            nk = min(6, 65 - 6 * g)
                            DMA("sp", kg[b][:, 0:nk, :, :], Kd[ch, :, g * 6:g * 6 + nk, :, :], [("Kd", ch, g)], [("kg", b)])
                        if p + 3 < NP2:
                            s3(p + 3)
                        qb = p % 2
                        pX = psF[p % 4]
                        kX = ("psF", p % 4)
                        Xv = pX[:, 0:nj * 256].rearrange("p (j r c) -> p j r c", j=nj, r=2)
                        Xr, Xi = Xv[:, :, 0, :], Xv[:, :, 1, :]
                        Kv = kg[b][:, kk:kk + nj, :, :]
                        Kr, Ki = Kv[:, :, 0, :], Kv[:, :, 1, :]
                        tqk = [("tq", qb, i) for i in range(4)]
                        TT(tq[qb][:, 0, 0:nj, :], Xr, Kr, ALU.mult, [kX, ("kg", b)], [tqk[0]])
                        STT(tq[qb][:, 1, 0:nj, :], Xi, -1.0, Ki, ALU.mult, ALU.mult, [kX, ("kg", b)], [tqk[1]])
                        TT(tq[qb][:, 2, 0:nj, :], Xr, Ki, ALU.mult, [kX, ("kg", b)], [tqk[2]])
                        TT(tq[qb][:, 3, 0:nj, :], Xi, Kr, ALU.mult, [kX, ("kg", b)], [tqk[3]])
                        pBq = psF[4 + p % 2]
                        kB = ("psF", 4 + p % 2)
                        grp = []
                        for j in range(nj):
                            k1 = k1a + j
                            G0, G1, G2 = Gres[:, k1, 0, :], Gres[:, k1, 1, :], Gres[:, k1, 2, :]
                            t = [tq[qb][:, i, j, :] for i in range(4)]
                            grp.append((pBq[:, j * 256:j * 256 + 128], [(G0, t[0]), (G0, t[1]), (G2, t[2]), (G2, t[3])]))
                            grp.append((pBq[:, j * 256 + 128:j * 256 + 256], [(G1, t[0]), (G1, t[1]), (G0, t[2]), (G0, t[3])]))
                        MMS(grp, ["G"] + tqk, [kB])
                        Bv = pBq[:, 0:nj * 256].rearrange("p (j r c) -> p j r c", j=nj, r=2)
                        ACT_COPY(Bp[:, k1a:k1a + nj, :], Bv[:, :, 0, :], [kB], [("Bp", k1a), ("Bp", k1a + 1)])
                        js = [j for j in range(nj) if 1 <= k1a + j <= 63]
                        if js:
                            j0, j1 = js[0], js[-1] + 1
                            ACT_COPY(Bp[:, 64 + k1a + j0:64 + k1a + j1, :], Bv[:, j0:j1, 1, :], [kB],
                                     [("Bp", 64 + k1a + j) for j in js])
                    bpk = [("Bp", k) for k in range(129)]
                    for c8 in range(16):
                        b = c8 % 2

                        def trs(e, b=b, c8=c8):
                            ins = None
                            for i in range(8):
                                ins = e.transpose(psB[b][:, i * 128:(i + 1) * 128], Bp[:, :, c8 * 8 + i], ident[:])
                            return ins
                        P.add("pe", trs, bpk + ["ident"], [("psB", b)])
                        (CP if c8 % 2 == 0 else ACT_COPY)(BT[:, c8 * 8:(c8 + 1) * 8, :], psB[b][:, :].rearrange("p (c n) -> p c n", c=8),
                                                         [("psB", b)], [("BT", c8)])
                    for c8 in range(16):
                        b = c8 % 3
                        for i in range(2):
                            c0 = c8 * 8 + i * 4
                            py = psF[(c8 * 2 + i) % 6]
                            ky = ("psF", (c8 * 2 + i) % 6)
                            MM(py[0:64, :], [(Em[:, 0:64], BT[:, c0:c0 + 4, :])], ["Em", ("BT", c8)] + AA_KEYS, [ky])
                            ACT(yT[b][:, i * 4:(i + 1) * 4, :], py[0:64, :].rearrange("p (c n) -> p c n", c=4), AF.Copy,
                                [ky], [("yT", b)])
                        DMA("pool", yd[ch * 128 + c8 * 8:ch * 128 + (c8 + 1) * 8, :].rearrange("c (a b) -> a c b", b=128),
                            yT[b][:], [("yT", b)], [("yd", ch, c8)])
            P.barrier()

        def ph4b():
            with ExitStack() as ph:
                def sbp(name, shape, dt):
                    return ph.enter_context(nc.sbuf_tensor(name, list(shape), dt))
                fy = [sbp("fy%d" % i, [128, 1024], F32) for i in range(2)]
                fu = [sbp("fu%d" % i, [128, 1024], BF16) for i in range(3)]
                fx = [sbp("fx%d" % i, [128, 1024], BF16) for i in range(4)]
                fs = [sbp("fs%d" % i, [128, 1024], BF16) for i in range(6)]
                fa = [sbp("fa%d" % i, [128, 1024], F32) for i in range(2)]
                f2_ = [sbp("f2_%d" % i, [128, 1024], F32) for i in range(2)]
                fb = [sbp("fb%d" % i, [128, 1024], F32) for i in range(3)]
                fq = [sbp("fq%d" % i, [128, 1024], BF16) for i in range(2)]
                fz = [sbp("fz%d" % i, [128, 1024], BF16) for i in range(2)]
                P.pipe_begin()
                for ch in range(6):
                    for j in range(8):
                        T_ = ch * 8 + j
                        b = T_ % 2
                        sl = slice(j * 1024, (j + 1) * 1024)
                        rows = slice(ch * 128, (ch + 1) * 128)
                        b3, b4, b6 = T_ % 3, T_ % 4, T_ % 6
                        P.at(T_, 0)
                        DMA("sp", fy[b][:], yd[rows, sl], [], [("fy", b)])
                        DMA("sp", fu[b3][:], ud[rows, sl], [], [("fu", b3)])
                        DMA("sp", fx[b4][:], x0d[rows, sl], [], [("fx", b4)])
                        DMA("sp", fs[b6][:], sgd[rows, sl], [], [("fs", b6)])
                        P.at(T_, 1)
                        ACT(fa[b][:], fy[b][:], AF.Identity, [("fy", b), "invn"], [("fa", b)], scale=invn[:, ch:ch + 1])
                        P.at(T_, 2)
                        STT(f2_[b][:], fu[b3][:], pc("hskip", ch), fa[b][:], ALU.mult, ALU.add, [("fu", b3), ("fa", b), "prm"], [("f2", b)])
                        P.at(T_, 3)
                        TT(fb[b3][:], f2_[b][:], fx[b4][:], ALU.mult, [("f2", b), ("fx", b4)], [("fb", b3)], eng="pool")
                        P.at(T_, 4)
                        ACT(fq[b][:], fb[b3][:], AF.Square, [("fb", b3)], [("fq", b)])
                        pss = psF[T_ % 6]
                        ks = ("psF", T_ % 6)
                        MMS([(pss[:, s:s + 1], [(fq[b][:, s * 128:(s + 1) * 128], ones[:, 0:1])]) for s in range(8)],
                            [("fq", b), "ones"], [ks])
                        P.at(T_, 5)
                        if ch == 0:
                            CP(ss[:, 0, j * 8:(j + 1) * 8], pss[:, 0:8], [ks], [("ss0", j)])
                        else:
                            TT(ss[:, 0, j * 8:(j + 1) * 8], ss[:, 0, j * 8:(j + 1) * 8], pss[:, 0:8], ALU.add,
                               [ks, ("ss0", j)], [("ss0", j)])
                        STT(fz[b][:], fb[b3][:], pc("hg", ch), fs[b6][:], ALU.mult, ALU.mult, [("fb", b3), ("fs", b6), "prm"], [("fz", b)])
                        P.at(T_, 6)
                        DMA("pool", zd[rows, sl], fz[b][:], [("fz", b)], [("zd", ch, 2 * j), ("zd", ch, 2 * j + 1)])
                P.pipe_end()
            P.barrier()

        def ph5():
            with ExitStack() as ph:
                def sbp(name, shape, dt):
                    return ph.enter_context(nc.sbuf_tensor(name, list(shape), dt))
                wos = sbp("wos", [128, DM], F32)
                wo = sbp("wo", [128, 12, DM], BF16)
                fgb = sbp("fgb", [128, DM], F32)
                rs = sbp("rs", [128, 2, 64], F32)
                zt = [sbp("zt5%d" % i, [128, 12, 512], BF16) for i in range(2)]
                xr = [sbp("xr%d" % i, [128, DM], F32) for i in range(3)]
                o1 = [sbp("o1%d" % i, [128, DM], F32) for i in range(2)]
                o2 = [sbp("o2%d" % i, [128, DM], F32) for i in range(2)]
                o3 = [sbp("o3%d" % i, [128, DM], F32) for i in range(2)]
                ojk = sbp("ojk", [128, DM], F32)
                s5 = sbp("s5", [128, 64], F32)
                r5 = sbp("r5", [128, 64], F32)
                DMA("sp", fgb[:], fg_d[:, :], [], ["fgb"])
                for cc in range(12):
                    DMA("sp", wos[:], w_out[cc * 128:(cc + 1) * 128, :], [], ["wos"])
                    CP(wo[:, cc, :], wos[:], ["wos"], ["wo"])
                ssk = [("ss0", j) for j in range(8)] + [("ss1", j) for j in range(16)]
                TS(rs[:], ss[:], 1.0 / DH, EPS, ALU.mult, ALU.add, ssk, ["rs"])
                ACT(rs[:], rs[:], AF.Sqrt, ["rs"], ["rs"])
                P.add("dve", lambda e: e.reciprocal(out=rs[:], in_=rs[:]), ["rs"], ["rs"])
                P.pipe_begin()
                for g in range(16):
                    gb = g % 2
                    zk = [("zt5", gb, c0) for c0 in (0, 3, 6, 9)]
                    for s in range(4):
                        tt = g * 4 + s
                        b = tt % 2
                        P.at(tt, 0)
                        if s == 0:
                            for g2 in ([0, 1] if g == 0 else [g + 1]):
                                if g2 < 16:
                                    for br in range(2):
                                        for half in range(2):
                                            c0 = br * 6 + half * 3
                                            DMA("sp", zt[g2 % 2][:, c0:c0 + 3, :],
                                                zd[c0 * 128:(c0 + 3) * 128, g2 * 512:(g2 + 1) * 512].rearrange("(cc p) t -> p cc t", p=128),
                                                [], [("zt5", g2 % 2, c0)])
                        DMA("sp", xr[tt % 3][:], x[tt * 128:(tt + 1) * 128, :], [], [("xr", tt % 3)])
                        tsl = slice(s * 128, (s + 1) * 128)
                        for hh_ in range(2):
                            pp = (2 * tt + hh_) % 3
                            pa, pbk = psF[2 * pp], psF[2 * pp + 1]
                            ka, kb_ = ("psF", 2 * pp), ("psF", 2 * pp + 1)
                            esl = slice(hh_ * 512, (hh_ + 1) * 512)
                            MM(pa[:, :], [(zt[gb][:, cc, tsl], wo[:, cc, esl]) for cc in range(6)], zk + ["wo"], [ka])
                            MM(pbk[:, :], [(zt[gb][:, 6 + cc, tsl], wo[:, 6 + cc, esl]) for cc in range(6)], zk + ["wo"], [kb_])
                            ACT(o1[b][:, esl], pa[:, :], AF.Identity, [ka, "rs"], [("o1", b, hh_)], scale=rs[:, 0, tt:tt + 1])
                            STT(o1[b][:, esl], pbk[:, :], rs[:, 1, tt:tt + 1], o1[b][:, esl], ALU.mult, ALU.add,
                                [kb_, ("o1", b, hh_), "rs"], [("o1", b, hh_)])
                        P.at(tt, 1)
                        TT(o2[b][:], o1[b][:], xr[tt % 3][:], ALU.add, [("o1", b, 0), ("o1", b, 1), ("xr", tt % 3)], [("o2", b)], eng="pool")
                        ACT(ojk[:], o2[b][:], AF.Square, [("o2", b)], ["ojk", ("s5", tt)], accum=s5[:, tt:tt + 1])
                        P.at(tt, 2)
                        TS(r5[:, tt:tt + 1], s5[:, tt:tt + 1], 1.0 / DM, EPS, ALU.mult, ALU.add, [("s5", tt)], [("r5", tt)])
                        ACT(r5[:, tt:tt + 1], r5[:, tt:tt + 1], AF.Sqrt, [("r5", tt)], [("r5", tt)])
                        P.add("dve", lambda e, tt=tt: e.reciprocal(out=r5[:, tt:tt + 1], in_=r5[:, tt:tt + 1]), [("r5", tt)], [("r5", tt)])
                        STT(o3[b][:], o2[b][:], r5[:, tt:tt + 1], fgb[:], ALU.mult, ALU.mult,
                            [("o2", b), ("r5", tt), "fgb"], [("o3", b)])
                        DMA("pool", out_d[tt * 128:(tt + 1) * 128, :], o3[b][:], [("o3", b)], [("out", tt)])
                P.pipe_end()
            P.barrier()

        if 1 in phases:
            ph1()
        if 2 in phases:
            ph2()
        if 3 in phases:
            ph3()
        if 0 in phases:
            ph0ab()
        if 0 in phases or 4 in phases:
            with ExitStack() as phm:
                Mres = phm.enter_context(nc.sbuf_tensor("Mres", [128, 65, 3, 128], BF16))
                F1 = phm.enter_context(nc.sbuf_tensor("F1", [128, 130], BF16))
                for h_ in range(5):
                    DMA("sp", Mres[:, h_ * 13:(h_ + 1) * 13, :, :], c_M[:, h_ * 13:(h_ + 1) * 13, :, :], [], ["M"])
                DMA("sp", F1[:], c_F1[:, :], [], ["F1"])
                if 0 in phases:
                    ph0c(Mres, F1)
                if 4 in phases:
                    ph4a(Mres, F1)
        if 4 in phases:
            ph4b()
        if debug:
            DMA("pool", ssd.rearrange("a p t -> p a t"), ss[:], [], ["ssd"])
            P.barrier()
        if 5 in phases:
            ph5()

        P.emit(nc, es)
    return nc


NGRP = 43
AA_KEYS = [("Aa", g) for g in range(NGRP)]


def fft_s1(MM, ACT, CP, psF, dT, F1, Aa, K, key, pre=None):
    c = 0
    g = 0
    while c < 128:
        n = min(3, 128 - c)
        pst = psF[g % 6]
        pk = ("psF", g % 6)
        for i in range(n):
            if pre is not None:
                pre(c + i)
            MM(pst[:, i * 130:(i + 1) * 130], [(dT(c + i), F1[0:K, 0:130])], [key(c + i), "F1"], [pk])
        o = Aa[:, :, c:c + n].rearrange("p k c -> p c k")
        i_ = pst[:, 0:n * 130].rearrange("p (c k) -> p c k", c=n)
        if g % 2 == 0:
            ACT(o, i_, mybir.ActivationFunctionType.Copy, [pk], [("Aa", g)])
        else:
            CP(o, i_, [pk], [("Aa", g)])
        c += n
        g += 1


def fft_s3(MMS, pX, Mres, mk, Aa, k1, R, W):
    Mr, Mi, nMi = Mres[:, mk, 0, :], Mres[:, mk, 1, :], Mres[:, mk, 2, :]
    Ar, Ai = Aa[:, k1, :], Aa[:, 65 + k1, :]
    MMS([(pX[:, 0:128], [(Mr, Ar), (nMi, Ai)]), (pX[:, 128:256], [(Mi, Ar), (Mr, Ai)])], R, W)


def conv3(ACT, STT, TT, DMA, raw3, tc_, tv_, tu_, tx_, pc, ud, x0d, ch, j):
    b = j % 2
    sl = slice(j * 512, (j + 1) * 512)
    rows = slice(ch * 128, (ch + 1) * 128)
    AFT = mybir.ActivationFunctionType
    MUL, ADD = mybir.AluOpType.mult, mybir.AluOpType.add

    def rd(q):
        return [("raw3", q, j), "rawpad3", "prm"] + ([("raw3", q, j + 1)] if j < 15 else []) + ([("raw3", q, j - 1)] if j > 0 else [])

    def w(q, k):
        return pc("hcw", k * 18 + q * 6 + ch)

    def tap(q, k):
        return raw3[q][:, j * 512 + k:j * 512 + k + 512]
    outs = [tv_[b], tx_[b], tc_[2][b]]
    okeys = [("tv", b), ("tx", b), ("tc", 2, b)]
    for q in range(3):
        t = tc_[q][b]
        tk = ("tc", q, b)
        ACT(t[:], tap(q, 1), AFT.Identity, rd(q), [tk], bias=pc("hcb", q * 6 + ch), scale=w(q, 1))
        STT(t[:], tap(q, 0), w(q, 0), t[:], MUL, ADD, rd(q) + [tk], [tk])
        STT(outs[q][:], tap(q, 2), w(q, 2), t[:], MUL, ADD, rd(q) + [tk], [okeys[q]])
        if q == 1:
            DMA("pool", x0d[rows, sl], tx_[b][:], [("tx", b)], [("x0d", ch, j)])
    TT(tu_[b][:], tc_[2][b][:], tv_[b][:], MUL, [("tc", 2, b), ("tv", b)], [("tu", b)])
    DMA("pool", ud[rows, sl], tu_[b][:], [("tu", b)], [("ud", ch, j)])


_NC = {}


def _in_map(inp, b):
    c = _constants()
    m = dict(c)
    m["x"] = np.ascontiguousarray(inp["x"][b])
    m["w_in"] = np.ascontiguousarray(inp["w_in"][0])
    m["w_out"] = np.ascontiguousarray(inp["w_out"][0])
    m["prm"] = _pack_params(inp)
    m["bd"] = _blockdiag(inp)
    m["fg"] = np.ascontiguousarray(np.broadcast_to(inp["final_g"][None, :], (128, DM))).astype(np.float32)
    m["flt_w1"] = np.ascontiguousarray(inp["flt_w1"][0])
    m["flt_w2"] = np.ascontiguousarray(inp["flt_w2"][0])
    m["flt_w3"] = np.ascontiguousarray(inp["flt_w3"][0])
    m["flt_w4"] = np.ascontiguousarray(inp["flt_w4"][0])
    return m


def kernel(**inputs):
    inp = {k: np.asarray(v) for k, v in inputs.items()}
    if "nc" not in _NC:
        _NC["nc"] = build()
    nc = _NC["nc"]
    in_maps = [_in_map(inp, b) for b in range(8)]
    res = run_bass_kernel_spmd(nc, in_maps, core_ids=list(range(8)))
    return np.stack([np.asarray(r["out"]).reshape(L, DM) for r in res.results], axis=0).astype(np.float32)
```

```python
import math
from contextlib import ExitStack

import numpy as np
import ml_dtypes
import concourse.bass as bass
import concourse.mybir as mybir
from concourse.bass_utils import run_bass_kernel_spmd

F32, BF16 = mybir.dt.float32, mybir.dt.bfloat16
AF = mybir.ActivationFunctionType
ALU = mybir.AluOpType

L = 8192
DM = 1024
NFFT = 16384
DH = 768
EPS = 1e-6
MIN_DECAY = math.log(1e-2) / 0.3
MAX_DECAY = math.log(1e-2) / 1.5

_cols = {}
_n = 0
for _name, _w in [("ng", 8), ("hcw", 54), ("hcb", 18), ("lcw", 24), ("lcb", 6), ("lba", 12), ("lbx", 12),
                  ("lam", 12), ("hskip", 6), ("hg", 6), ("lg", 6), ("flt", 6), ("nd", 6)]:
    _cols[_name] = _n
    _n += _w
NPRM = _n


class Prog:
    ENGS = ["pe", "act", "dve", "pool", "sp"]

    def __init__(self):
        self.ops = []
        self.lw = {}
        self.rd = {}

    def pipe_begin(self):
        self._pipe = {}
        self._cur = (0, 0)

    def at(self, tile, stage):
        self._cur = (tile, stage)

    def pipe_end(self):
        slots = self._pipe
        self._pipe = None
        for k in sorted(slots, key=lambda ts: (ts[0] + ts[1], ts[1])):
            for op in slots[k]:
                self.add(*op)

    def pipe_steps(self):
        slots = self._pipe
        self._pipe = None
        keys = sorted(slots, key=lambda ts: (ts[0] + ts[1], ts[1]))
        cur = None
        for k in keys:
            t = k[0] + k[1]
            if cur is not None and t != cur:
                yield
            cur = t
            for op in slots[k]:
                self.add(*op)
        yield

    def add(self, eng, fn, reads=(), writes=(), dma=False):
        if getattr(self, "_pipe", None) is not None:
            self._pipe.setdefault(self._cur, []).append((eng, fn, list(reads), list(writes), dma))
            return -1
        idx = len(self.ops)
        deps = set()
        for k in reads:
            if k in self.lw:
                deps.add(self.lw[k])
        for k in writes:
            if k in self.lw:
                deps.add(self.lw[k])
            for r in self.rd.get(k, ()):
                deps.add(r)
        for k in reads:
            self.rd.setdefault(k, []).append(idx)
        for k in writes:
            self.lw[k] = idx
            self.rd[k] = []
        deps.discard(idx)
        self.ops.append(dict(eng=eng, fn=fn, deps=deps, dma=dma, sig=0, dsem=None, dprev=None))
        return idx

    def barrier(self):
        last = {}
        dmas = []
        for i, op in enumerate(self.ops):
            if op["dma"]:
                dmas.append(i)
            else:
                last[op["eng"]] = i
        start = getattr(self, "_bar_from", 0)
        deps = set(last.values()) | {i for i in dmas if i >= start}
        self._bar_from = len(self.ops)
        for e in self.ENGS:
            idx = len(self.ops)
            self.ops.append(dict(eng=e, fn=None, deps=set(deps), dma=False, sig=0, dsem=None, dprev=None))
            last[e] = idx
        self.lw = {}
        self.rd = {}

    def emit(self, nc, es, npool=16):
        ops = self.ops
        sem_eng = {e: es.enter_context(nc.semaphore("s_" + e)) for e in self.ENGS}
        dma_sems = {e: [es.enter_context(nc.semaphore("d_%s%d" % (e, i))) for i in range(npool)]
                    for e in ("sp", "pool")}
        has_dep = set()
        for op in ops:
            has_dep |= op["deps"]
        cnt = {e: 0 for e in self.ENGS}
        rr = {e: 0 for e in dma_sems}
        uses = {e: [0] * npool for e in dma_sems}
        lastop = {e: [None] * npool for e in dma_sems}
        for i, op in enumerate(ops):
            e = op["eng"]
            if op["dma"]:
                s = rr[e] % npool
                rr[e] += 1
                uses[e][s] += 1
                op["dsem"] = (dma_sems[e][s], 16 * uses[e][s])
                op["dprev"] = lastop[e][s]
                lastop[e][s] = i
            elif i in has_dep and op["fn"] is not None:
                cnt[e] += 1
                op["sig"] = cnt[e]
        per_eng = {e: [i for i, op in enumerate(ops) if op["eng"] == e] for e in self.ENGS}

        def run(e, eo):
            known = {}
            for i in per_eng[e]:
                op = ops[i]
                waits = {}
                deps = set(op["deps"])
                if op["dprev"] is not None:
                    deps.add(op["dprev"])
                for d in deps:
                    D = ops[d]
                    if D["fn"] is None:
                        continue
                    if D["dma"]:
                        s, v = D["dsem"]
                    else:
                        if D["eng"] == e and e == "pe":
                            continue
                        s, v = sem_eng[D["eng"]], D["sig"]
                        assert v > 0
                    key = id(s)
                    if known.get(key, 0) >= v:
                        continue
                    if key not in waits or waits[key][1] < v:
                        waits[key] = (s, v)
                for key, (s, v) in waits.items():
                    eo.wait_ge(s, v)
                    known[key] = v
                if op["fn"] is None:
                    continue
                ins = op["fn"](eo)
                if op["dma"]:
                    ins.then_inc(op["dsem"][0], 16)
                elif op["sig"]:
                    ins.then_inc(sem_eng[e], 1)
            if e in dma_sems:
                for s in range(npool):
                    if uses[e][s]:
                        eo.wait_ge(dma_sems[e][s], 16 * uses[e][s])

        block = es.enter_context(nc.Block())

        @block.tensor
        def _(eo):
            run("pe", eo)

        @block.scalar
        def _(eo):
            run("act", eo)

        @block.vector
        def _(eo):
            run("dve", eo)

        @block.gpsimd
        def _(eo):
            run("pool", eo)

        @block.sync
        def _(eo):
            run("sp", eo)


def _bf(a):
    return np.ascontiguousarray(a.astype(np.float32)).astype(ml_dtypes.bfloat16)


_CONST = None


def _constants():
    global _CONST
    if _CONST is not None:
        return _CONST
    c = {}
    c["c_ident"] = _bf(np.eye(128))
    n = np.arange(NFFT)
    ti = np.where(n < L, n, NFFT - n)
    ti[L] = 0
    t = np.linspace(0.0, 1.0, L, dtype=np.float32)
    w = ((2.0 * math.pi / L) * np.arange(L, dtype=np.float32)).astype(np.float32)
    f = np.linspace(1e-4, 15, 16, dtype=np.float32)
    wf = (w[:, None] * f[None, :]).astype(np.float32)
    z = np.concatenate([t[:, None], np.cos(wf), -np.sin(wf)], axis=-1).astype(np.float32)
    c["c_zc"] = np.ascontiguousarray(z[ti].T)
    c["c_trow"] = np.ascontiguousarray(np.broadcast_to(t[ti][None, :], (128, NFFT))).astype(np.float32)
    n1 = np.arange(128, dtype=np.float64)[:, None]
    k1 = np.arange(65, dtype=np.float64)[None, :]
    ang = 2 * np.pi * n1 * k1 / 128
    c["c_F1"] = _bf(np.concatenate([np.cos(ang), -np.sin(ang)], axis=1))
    n2 = np.arange(128, dtype=np.float64)[:, None, None]
    kk1 = np.arange(65, dtype=np.float64)[None, :, None]
    k2 = np.arange(128, dtype=np.float64)[None, None, :]
    th = 2 * np.pi * n2 * (kk1 + 128 * k2) / NFFT
    Mr, Mi = np.cos(th), -np.sin(th)
    c["c_M"] = _bf(np.stack([Mr, Mi, -Mi], axis=2))
    thT = np.transpose(th, (2, 1, 0))
    Gr, Gi = np.cos(thT), np.sin(thT)
    c["c_G"] = _bf(np.stack([Gr, Gi, -Gi], axis=2))
    nn1 = np.arange(64, dtype=np.float64)[None, :]
    rows = []
    for k in range(65):
        ck = 1.0 if k in (0, 64) else 2.0
        rows.append(ck * np.cos(2 * np.pi * nn1 * k / 128) / NFFT)
    for k in range(1, 64):
        rows.append(-2.0 * np.sin(2 * np.pi * nn1 * k / 128) / NFFT)
    c["c_E"] = _bf(np.concatenate(rows, axis=0))
    _CONST = c
    return c


def _pack_params(inp):
    p = np.zeros((128, NPRM), np.float32)

    def put(name, arr):
        p[:, _cols[name]:_cols[name] + arr.shape[1]] = arr

    put("ng", inp["norm_g"][0].reshape(8, 128).T)
    hcw = inp["hy_conv_w"][0]
    put("hcw", hcw.reshape(3, 18, 128).transpose(2, 0, 1).reshape(128, 54))
    put("hcb", inp["hy_conv_b"][0].reshape(18, 128).T)
    put("lcw", inp["lru_conv_w"][0].reshape(4, 6, 128).transpose(2, 0, 1).reshape(128, 24))
    put("lcb", inp["lru_conv_b"][0].reshape(6, 128).T)
    put("lba", inp["lru_ba"][0].reshape(2, 6, 128).transpose(2, 0, 1).reshape(128, 12))
    put("lbx", inp["lru_bx"][0].reshape(2, 6, 128).transpose(2, 0, 1).reshape(128, 12))
    put("lam", inp["lru_lam"][0].reshape(2, 6, 128).transpose(2, 0, 1).reshape(128, 12))
    put("hskip", inp["hy_skip"][0].reshape(6, 128).T)
    put("hg", inp["hy_out_g"][0].reshape(6, 128).T)
    put("lg", inp["lru_out_g"][0].reshape(6, 128).T)
    flt = np.zeros((128, 6), np.float32)
    for li, (fk, bk) in enumerate([("flt_f1", "flt_b1"), ("flt_f2", "flt_b2"), ("flt_f3", "flt_b3")]):
        flt[:64, 2 * li] = inp[fk][0]
        flt[:64, 2 * li + 1] = inp[bk][0]
    put("flt", flt)
    deltas = np.linspace(MIN_DECAY, MAX_DECAY, DH, dtype=np.float32)
    put("nd", (-np.abs(deltas)).reshape(6, 128).T)
    return p


def _blockdiag(inp):
    out = np.zeros((2, 2, 6, 128, 128), np.float32)
    for d in range(2):
        for gi, key in enumerate(["lru_wa", "lru_wx"]):
            wgt = inp[key][0, d]
            for ch in range(6):
                out[d, gi, ch, 0:64, 0:64] = wgt[2 * ch]
                out[d, gi, ch, 64:128, 64:128] = wgt[2 * ch + 1]
    return out


def build(debug=False, phases=(0, 1, 2, 3, 4, 5), lim=99, nch=6):
    nc = bass.Bass("TRN2", target_bir_lowering=False)
    P = Prog()

    def din(name, shape, dt=F32):
        return nc.dram_tensor(name, list(shape), dt, kind="ExternalInput").ap()

    def dscr(name, shape, dt):
        return nc.dram_tensor(name, list(shape), dt, kind=("ExternalOutput" if debug else "Internal")).ap()

    x = din("x", [L, DM])
    w_in = din("w_in", [DM, 4608])
    w_out = din("w_out", [1536, DM])
    prm_d = din("prm", [128, NPRM])
    bd_d = din("bd", [2, 2, 6, 128, 128])
    fg_d = din("fg", [128, DM])
    w1_d = din("flt_w1", [33, 64])
    w2_d = din("flt_w2", [64, 64])
    w3_d = din("flt_w3", [64, 64])
    w4_d = din("flt_w4", [64, 1536])
    c_ident = din("c_ident", [128, 128], BF16)
    c_zc = din("c_zc", [33, NFFT])
    c_trow = din("c_trow", [128, NFFT])
    c_F1 = din("c_F1", [128, 130], BF16)
    c_M = din("c_M", [128, 65, 3, 128], BF16)
    c_G = din("c_G", [128, 65, 3, 128], BF16)
    c_E = din("c_E", [128, 64], BF16)
    out_d = nc.dram_tensor("out", [L, DM], F32, kind="ExternalOutput").ap()

    xnTd = dscr("xnTd", [8, 128, L], BF16)
    kd = dscr("kd", [DH, NFFT], BF16)
    Kd = dscr("Kd", [6, 128, 65, 2, 128], F32)
    ud = dscr("ud", [DH, L], BF16)
    x0d = dscr("x0d", [DH, L], BF16)
    sgd = dscr("sgd", [DH, L], BF16)
    yd = dscr("yd", [DH, L], F32)
    zd = dscr("zd", [1536, L], BF16)
    ssd = dscr("ssd", [2, 128, 64], F32)

    es = ExitStack()
    with es:
        def sb(name, shape, dt):
            return es.enter_context(nc.sbuf_tensor(name, list(shape), dt))

        def ACT(out, in_, func, R, W, bias=None, scale=None, accum=None):
            kw = {}
            if bias is not None:
                kw["bias"] = bias
            if scale is not None:
                kw["scale"] = scale
            if accum is not None:
                kw["accum_out"] = accum
            P.add("act", lambda e: e.activation(out=out, in_=in_, func=func, **kw), R, W)

        def TS(out, in0, s1, s2, op0, op1, R, W, eng="dve", accum=None):
            kw = {}
            if accum is not None:
                kw["accum_out"] = accum
            if op1 is None:
                P.add(eng, lambda e: e.tensor_scalar(out=out, in0=in0, scalar1=s1, scalar2=None, op0=op0, **kw), R, W)
            else:
                P.add(eng, lambda e: e.tensor_scalar(out=out, in0=in0, scalar1=s1, scalar2=s2, op0=op0, op1=op1, **kw), R, W)

        def TT(out, in0, in1, op, R, W, eng="dve"):
            P.add(eng, lambda e: e.tensor_tensor(out=out, in0=in0, in1=in1, op=op), R, W)

        def STT(out, in0, scalar, in1, op0, op1, R, W):
            P.add("dve", lambda e: e.scalar_tensor_tensor(out=out, in0=in0, scalar=scalar, in1=in1, op0=op0, op1=op1), R, W)

        def CP(out, in_, R, W, eng="dve"):
            P.add(eng, lambda e: e.tensor_copy(out=out, in_=in_), R, W)

        def ACT_COPY(out, in_, R, W):
            ACT(out, in_, AF.Copy, R, W)

        def MM(out, pairs, R, W):
            pairs = list(pairs)

            def f(e):
                ins = None
                for i, (l, r) in enumerate(pairs):
                    ins = e.matmul(out, l, r, start=(i == 0), stop=(i == len(pairs) - 1))
                return ins
            P.add("pe", f, R, W)

        def MMS(outs_pairs, R, W):
            groups = [(o, list(p)) for o, p in outs_pairs]

            def f(e):
                ins = None
                for o, pairs in groups:
                    for i, (l, r) in enumerate(pairs):
                        ins = e.matmul(o, l, r, start=(i == 0), stop=(i == len(pairs) - 1))
                return ins
            P.add("pe", f, R, W)

        def DMA(q, out, in_, R, W):
            P.add(q, lambda e: e.dma_start(out=out, in_=in_), R, W, dma=True)

        def MSET(ap, val, R, W, eng="dve"):
            P.add(eng, lambda e: e.memset(ap, val), R, W)

        ident = sb("ident", [128, 128], BF16)
        ones = sb("ones", [128, 2], BF16)
        prm = sb("prm_s", [128, NPRM], F32)
        drv = sb("drv", [128, 64], F32)
        invn = sb("invn", [128, 8], F32)
        ss = sb("ss", [128, 2, 64], F32)
        psF = [es.enter_context(nc.psum_tensor("psF%d" % i, [128, 512], F32)) for i in range(6)]
        psB = [es.enter_context(nc.psum_tensor("psB%d" % i, [128, 1024], BF16)) for i in range(2)]

        def pc(name, j=0, w=1):
            c0 = _cols[name] + j
            return prm[:, c0:c0 + w]

        DMA("sp", ident[:], c_ident[:, :], [], ["ident"])
        DMA("sp", prm[:], prm_d[:, :], [], ["prm"])
        MSET(ones[:], 1.0, [], ["ones"])
        ACT(drv[:, 0:12], pc("lam", 0, 12), AF.Exp, ["prm"], ["drv"], scale=-1.0)
        ACT(drv[:, 0:12], drv[:, 0:12], AF.Ln, ["drv"], ["drv"], bias=1.0)
        TS(drv[:, 12:24], drv[:, 0:12], -16.0, None, ALU.mult, None, ["drv"], ["drv"])
        TS(drv[:, 0:12], drv[:, 0:12], -8.0, None, ALU.mult, None, ["drv"], ["drv"])
        for li in range(3):
            TT(drv[:, 25 + 2 * li:26 + 2 * li], pc("flt", 2 * li), pc("flt", 2 * li + 1), ALU.mult, ["prm", "drv"], ["drv"])
            TS(drv[:, 25 + 2 * li:26 + 2 * li], drv[:, 25 + 2 * li:26 + 2 * li], 1.0 / 3.0, None, ALU.mult, None, ["drv"], ["drv"])
            TS(drv[:, 24 + 2 * li:25 + 2 * li], pc("flt", 2 * li), 1.0 / 3.0, None, ALU.mult, None, ["prm", "drv"], ["drv"])

        def wprep(wst, wb, blocks, tag):
            for q, c0 in enumerate(blocks):
                DMA("sp", wst[:, :, q * 128:(q + 1) * 128],
                    w_in[:, c0:c0 + 128].rearrange("(dc p) c -> p dc c", p=128), [], [(tag, "wst", q)])
                for dc in range(8):
                    TS(wb[:, dc, q * 128:(q + 1) * 128], wst[:, dc, q * 128:(q + 1) * 128], pc("ng", dc), None,
                       ALU.mult, None, [(tag, "wst", q), "prm"], [(tag, "wb")])

        def ph0ab():
            with ExitStack() as ph:
                def sbp(name, shape, dt):
                    return ph.enter_context(nc.sbuf_tensor(name, list(shape), dt))
                w1 = sbp("w1", [33, 64], F32)
                w2 = sbp("w2", [64, 64], F32)
                w3 = sbp("w3", [64, 64], F32)
                w4 = sbp("w4", [64, 1536], F32)
                h3T = sbp("h3T", [64, NFFT], F32)
                trow = sbp("trow", [128, NFFT], F32)
                zt = [sbp("zt%d" % i, [33, 512], F32) for i in range(2)]
                hs = [sbp("hs%d" % i, [64, 512], F32) for i in range(3)]
                hq = [sbp("hq%d" % i, [64, 512], F32) for i in range(3)]
                hh = [sbp("hh%d" % i, [64, 512], F32) for i in range(3)]
                DMA("sp", w1[:], w1_d[:, :], [], ["w1"])
                DMA("sp", w2[:], w2_d[:, :], [], ["w2"])
                DMA("sp", w3[:], w3_d[:, :], [], ["w3"])
                DMA("sp", w4[:], w4_d[:, :], [], ["w4"])
                DMA("sp", trow[:], c_trow[:, :], [], ["trow"])
                P.pipe_begin()
                for j in range(32):
                    b = j % 2
                    b3 = j % 3
                    P.at(j, 0)
                    DMA("sp", zt[b][:], c_zc[:, j * 512:(j + 1) * 512], [], [("zt", b)])
                    cur = zt[b][0:33, :]
                    curk = ("zt", b)
                    for li, (wl, kdim) in enumerate([(w1, 33), (w2, 64), (w3, 64)]):
                        P.at(j, li)
                        pst = psF[(3 * j + li) % 6]
                        pk = ("psF", (3 * j + li) % 6)
                        MM(pst[0:64, :], [(wl[0:kdim, 0:64], cur)], [curk, "w%d" % (li + 1)], [pk])
                        ACT(hs[b3][:], pst[0:64, :], AF.Sin, [pk, "drv"], [("hs", b3)],
                            bias=drv[0:64, 25 + 2 * li:26 + 2 * li], scale=drv[0:64, 24 + 2 * li:25 + 2 * li])
                        TT(hq[b3][:], hs[b3][:], hs[b3][:], ALU.mult, [("hs", b3)], [("hq", b3)])
                        TS(hq[b3][:], hq[b3][:], -4.0, 3.0, ALU.mult, ALU.add, [("hq", b3)], [("hq", b3)])
                        if li < 2:
                            TT(hh[b3][:], hq[b3][:], hs[b3][:], ALU.mult, [("hq", b3), ("hs", b3)], [("hh", b3)])
                            cur = hh[b3][:]
                            curk = ("hh", b3)
                        else:
                            TT(h3T[:, j * 512:(j + 1) * 512], hq[b3][:], hs[b3][:], ALU.mult,
                               [("hq", b3), ("hs", b3)], [("h3T", j)])
                P.pipe_end()
                wt = [sbp("wt%d" % i, [128, 512], F32) for i in range(2)]
                kt = [sbp("kt%d" % i, [128, 512], F32) for i in range(2)]
                kj = [sbp("kj%d" % i, [128, 512], F32) for i in range(2)]
                kb = [sbp("kb%d" % i, [128, 512], BF16) for i in range(2)]
                ksum = sbp("ksum", [128, 32], F32)
                kred = sbp("kred", [128, 1], F32)
                for ch in range(6):
                    P.pipe_begin()
                    for j in range(32):
                        b = j % 2
                        col0 = ch * 128 + (0 if j < 16 else DH)
                        pst = psF[j % 6]
                        pk = ("psF", j % 6)
                        P.at(j, 0)
                        MM(pst[:, :], [(w4[0:64, col0:col0 + 128], h3T[0:64, j * 512:(j + 1) * 512])], [("h3T", j), "w4"], [pk])
                        ACT(wt[b][:], trow[:, j * 512:(j + 1) * 512], AF.Exp, ["trow", "prm"], [("wt", b)], scale=pc("nd", ch))
                        P.at(j, 1)
                        TT(kt[b][:], pst[:, :], wt[b][:], ALU.mult, [pk, ("wt", b)], [("kt", b)])
                        if j == 16:
                            MSET(kt[b][:, 0:1], 0.0, [], [("kt", b)])
                        P.at(j, 2)
                        ACT(kj[b][:], kt[b][:], AF.Abs, [("kt", b)], [("kj", b), ("ksum", j)], accum=ksum[:, j:j + 1])
                        CP(kb[b][:], kt[b][:], [("kt", b)], [("kb", b)])
                        P.at(j, 3)
                        DMA("pool", kd[ch * 128:(ch + 1) * 128, j * 512:(j + 1) * 512], kb[b][:], [("kb", b)], [("kd", ch, j)])
                    P.pipe_end()
                    P.add("dve", lambda e: e.tensor_reduce(out=kred[:], in_=ksum[:], axis=mybir.AxisListType.X, op=ALU.add),
                          [("ksum", j) for j in range(32)], ["kred"])
                    TS(kred[:], kred[:], EPS, None, ALU.add, None, ["kred"], ["kred"])
                    P.add("dve", lambda e, ch=ch: e.reciprocal(out=invn[:, ch:ch + 1], in_=kred[:]), ["kred"], ["invn"])
            P.barrier()

        def ph0c(Mres, F1):
            with ExitStack() as ph:
                def sbp(name, shape, dt):
                    return ph.enter_context(nc.sbuf_tensor(name, list(shape), dt))
                kT = sbp("kT", [128, 128, 128], BF16)
                Aa = sbp("Aa", [128, 130, 128], BF16)
                kg = [sbp("kg%d" % i, [128, 6, 2, 128], F32) for i in range(2)]
                for ch in range(6):
                    for q in range(8):
                        DMA("sp", kT[:, q * 16:(q + 1) * 16, :],
                            kd[ch * 128 + q * 16:ch * 128 + (q + 1) * 16, :].rearrange("c (a b) -> a c b", b=128),
                            [("kd", ch, j) for j in range(32)], [("kT", q)])
                    fft_s1(MM, ACT, CP, psF, lambda c: kT[0:128, c, :], F1, Aa, 128, lambda c: ("kT", c // 16))
                    for g in range(11):
                        b = g % 2
                        nk = min(6, 65 - 6 * g)
                        for kk in range(nk):
                            k1 = g * 6 + kk
                            pst = psF[k1 % 6]
                            pk = ("psF", k1 % 6)
                            fft_s3(MMS, pst[:, 0:256], Mres, k1, Aa, k1, ["M"] + AA_KEYS, [pk])
                            ev = ACT_COPY if k1 % 2 == 0 else CP
                            ev(kg[b][:, kk, :, :], pst[:, 0:256].rearrange("p (r c) -> p r c", r=2), [pk], [("kg", b)])
                        DMA("pool", Kd[ch, :, g * 6:g * 6 + nk, :, :], kg[b][:, 0:nk, :, :], [("kg", b)], [("Kd", ch, g)])
            P.barrier()

        def ph1():
            with ExitStack() as ph:
                def sbp(name, shape, dt):
                    return ph.enter_context(nc.sbuf_tensor(name, list(shape), dt))
                NX = 6
                xin = [sbp("xin%d" % i, [128, DM], F32) for i in range(NX)]
                xjk = sbp("xjk", [128, DM], F32)
                xs = [sbp("xs%d" % i, [128, DM], BF16) for i in range(2)]
                xT = [sbp("xT%d" % i, [128, 8, 512], BF16) for i in range(2)]
                s1 = sbp("s1", [128, 64], F32)
                r1 = sbp("r1", [128, 64], F32)
                P.pipe_begin()
                for g in range(16):
                    gb = g % 2
                    for s in range(4):
                        tt = g * 4 + s
                        b = tt % 2
                        bx = tt % NX
                        P.at(tt, 0)
                        DMA("sp", xin[bx][:], x[tt * 128:(tt + 1) * 128, :], [], [("xin", bx)])
                        P.at(tt, 1)
                        ACT(xjk[:], xin[bx][:], AF.Square, [("xin", bx)], ["xjk", ("s1", tt)], accum=s1[:, tt:tt + 1])
                        P.at(tt, 2)
                        TS(r1[:, tt:tt + 1], s1[:, tt:tt + 1], 1.0 / DM, EPS, ALU.mult, ALU.add, [("s1", tt)], [("r1", tt)])
                        P.at(tt, 3)
                        ACT(r1[:, tt:tt + 1], r1[:, tt:tt + 1], AF.Sqrt, [("r1", tt)], [("r1", tt)])
                        P.at(tt, 4)
                        P.add("dve", lambda e, tt=tt: e.reciprocal(out=r1[:, tt:tt + 1], in_=r1[:, tt:tt + 1]), [("r1", tt)], [("r1", tt)])
                        P.at(tt, 5)
                        TS(xs[b][:], xin[bx][:], r1[:, tt:tt + 1], None, ALU.mult, None, [("xin", bx), ("r1", tt)], [("xs", b)])
                        P.at(tt, 6)

                        def trs(e, b=b):
                            ins = None
                            for dc in range(8):
                                ins = e.transpose(psB[b][:, dc * 128:(dc + 1) * 128], xs[b][:, dc * 128:(dc + 1) * 128], ident[:])
                            return ins
                        P.add("pe", trs, [("xs", b), "ident"], [("psB", b)])
                        P.at(tt, 7)
                        CP(xT[gb][:, :, s * 128:(s + 1) * 128], psB[b][:, :].rearrange("p (dc t) -> p dc t", dc=8),
                           [("psB", b)], [("xT", gb)])
                        if s == 3:
                            P.at(tt, 8)
                            DMA("pool", xnTd[:, :, g * 512:(g + 1) * 512].rearrange("dc p t -> p dc t"), xT[gb][:],
                                [("xT", gb)], [("xnTd", g)])
                P.pipe_end()
            P.barrier()

        def ph2():
            with ExitStack() as ph:
                def sbp(name, shape, dt):
                    return ph.enter_context(nc.sbuf_tensor(name, list(shape), dt))
                wst = sbp("wst", [128, 8, 256], F32)
                wb = sbp("wb", [128, 8, 256], BF16)
                xt = [sbp("xt%d" % i, [128, 8, 512], BF16) for i in range(2)]
                raw = sbp("raw", [128, L + 8], BF16)
                xb2 = [sbp("xb%d" % i, [128, L], BF16) for i in range(2)]
                sg2 = [sbp("sg%d" % i, [128, L], BF16) for i in range(2)]
                hf = sbp("hf", [128, L], BF16)
                ra = sbp("ra", [128, L], F32)
                ix = sbp("ix", [128, L], BF16)
                dg = sbp("dg", [128, 4, 128], BF16)
                bdf = sbp("bdf", [128, 4, 128], F32)
                bdb2 = [sbp("bdb%d" % i, [128, 4, 128], BF16) for i in range(2)]
                ti_ = [sbp("ti%d" % i, [128, 512], BF16) for i in range(2)]
                a2_ = [sbp("a2%d" % i, [128, 512], F32) for i in range(2)]
                tm_ = [sbp("tm%d" % i, [128, 512], BF16) for i in range(2)]
                tb_ = [sbp("tb%d" % i, [128, 512], BF16) for i in range(2)]
                hb_ = [sbp("hb%d" % i, [128, 512], F32) for i in range(2)]
                ty_ = [sbp("ty%d" % i, [128, 512], F32) for i in range(2)]
                tq_ = [sbp("tq%d" % i, [128, 512], BF16) for i in range(2)]
                tz_ = [sbp("tz%d" % i, [128, 512], BF16) for i in range(2)]
                tg2_ = [sbp("tg2%d" % i, [128, 512], BF16) for i in range(2)]
                MSET(raw[:, 0:1], 0.0, [], ["rawpad"])
                MSET(raw[:, L + 1:L + 8], 0.0, [], ["rawpad"])

                def X(ch):
                    cp = ch % 2
                    xb, sg, bdb = xb2[cp], sg2[cp], bdb2[cp]
                    wprep(wst, wb, [3072 + ch * 128, 3840 + ch * 128], "l")
                    for k in range(4):
                        TS(dg[:, k, :], ident[:], pc("lcw", k * 6 + ch), None, ALU.mult, None, ["ident", "prm"], ["dg"])
                    for d in range(2):
                        for gi in range(2):
                            DMA("sp", bdf[:, d * 2 + gi, :], bd_d[d, gi, ch, :, :], [], [("bdf", d * 2 + gi)])
                            CP(bdb[:, d * 2 + gi, :], bdf[:, d * 2 + gi, :], [("bdf", d * 2 + gi)], [("bdb", cp)])
                    yield

                    def conv(j):
                        pcv = psF[2]
                        kc = ("psF", 2)
                        rds = [("raw", j), "dg", "rawpad"] + ([("raw", j + 1)] if j < 15 else [])
                        MM(pcv[:, :], [(dg[:, k, :], raw[:, j * 512 + k:j * 512 + k + 512]) for k in range(4)], rds, [kc])
                        ACT(xb[:, j * 512:(j + 1) * 512], pcv[:, :], AF.Identity, [kc, "prm"], [("xb", cp, j)], bias=pc("lcb", ch))
                    for j in range(16):
                        b = j % 2
                        DMA("sp", xt[b][:], xnTd[:, :, j * 512:(j + 1) * 512].rearrange("dc p t -> p dc t"),
                            [], [("xt", b)])
                        i0, i1 = 0, 1
                        p0, p1 = psF[i0], psF[i1]
                        k0, k1_ = ("psF", i0), ("psF", i1)
                        MM(p0[:, :], [(wb[:, dc, 0:128], xt[b][:, dc, :]) for dc in range(8)], [("l", "wb"), ("xt", b)], [k0])
                        yield "s"
                        MM(p1[:, :], [(wb[:, dc, 128:256], xt[b][:, dc, :]) for dc in range(8)], [("l", "wb"), ("xt", b)], [k1_])
                        CP(raw[:, 1 + j * 512:1 + (j + 1) * 512], p0[:, :], [k0], [("raw", j)])
                        ACT(sg[:, j * 512:(j + 1) * 512], p1[:, :], AF.Copy, [k1_], [("sg", cp, j)])
                        yield "s"
                        if j > 0:
                            conv(j - 1)
                        yield
                    conv(15)
                    yield

                def Y(ch):
                    cp = ch % 2
                    xb, sg, bdb = xb2[cp], sg2[cp], bdb2[cp]
                    for d in range(2):
                        order = list(range(16)) if d == 0 else list(range(15, -1, -1))
                        for j in order:
                            b = j % 2
                            sl = slice(j * 512, (j + 1) * 512)
                            p0, p1 = psF[4], psF[5]
                            k0, k1_ = ("psF", 4), ("psF", 5)
                            MM(p0[:, :], [(bdb[:, d * 2, :], xb[:, sl])], [("bdb", cp), ("xb", cp, j)], [k0])
                            MM(p1[:, :], [(bdb[:, d * 2 + 1, :], xb[:, sl])], [("bdb", cp), ("xb", cp, j)], [k1_])
                            ACT(ra[:, sl], p0[:, :], AF.Sigmoid, [k0, "prm"], [("ra", j)], bias=pc("lba", d * 6 + ch))
                            ACT(ti_[b][:], p1[:, :], AF.Sigmoid, [k1_, "prm"], [("ti", b)], bias=pc("lbx", d * 6 + ch))
                            TT(ix[:, sl], ti_[b][:], xb[:, sl], ALU.mult, [("ti", b), ("xb", cp, j)], [("ix", j)])
                            if d == 0:
                                ACT(tg2_[b][:], sg[:, sl], AF.Sigmoid, [("sg", cp, j)], [("tg2", b)])
                                TT(sg[:, sl], sg[:, sl], tg2_[b][:], ALU.mult, [("sg", cp, j), ("tg2", b)], [("sg", cp, j)], eng="pool")
                            yield "A"
                        qs = range(4) if d == 0 else range(3, -1, -1)
                        for q in qs:
                            ks = [("ra", 4 * q + i) for i in range(4)]
                            ACT(ra[:, q * 2048:(q + 1) * 2048], ra[:, q * 2048:(q + 1) * 2048], AF.Exp, ks + ["drv"], ks,
                                scale=drv[:, d * 6 + ch:d * 6 + ch + 1])
                            yield "B"
                        P.pipe_begin()
                        for n_, j in enumerate(order):
                            b = j % 2
                            sl = slice(j * 512, (j + 1) * 512)
                            P.at(n_, 0)
                            ACT(a2_[b][:], ra[:, sl], AF.Square, [("ra", j)], [("a2", b)])
                            ACT(tm_[b][:], a2_[b][:], AF.Sqrt, [("a2", b)], [("tm", b)], bias=1.0, scale=-1.0)
                            P.at(n_, 1)
                            if d == 0 and j == 0:
                                MSET(tm_[b][:, 0:1], 1.0, [], [("tm", b)], eng="pool")
                            if d == 1 and j == 15:
                                MSET(tm_[b][:, 511:512], 1.0, [], [("tm", b)], eng="pool")
                            TT(tb_[b][:], ix[:, sl], tm_[b][:], ALU.mult, [("ix", j), ("tm", b)], [("tb", b)], eng="pool")
                            P.at(n_, 2)
                            if d == 0:
                                init = 0.0 if j == 0 else hf[:, j * 512 - 1:j * 512]
                                rd = [("ra", j), ("tb", b)] + ([("hf", j - 1)] if j > 0 else [])
                                P.add("dve", lambda e, b=b, sl=sl, init=init: e.tensor_tensor_scan(
                                    out=hf[:, sl], data0=ra[:, sl], data1=tb_[b][:], initial=init, op0=ALU.mult, op1=ALU.add),
                                    rd, [("hf", j)])
                            else:
                                init = 0.0 if j == 15 else hb_[1 - b][:, 0:1]
                                rd = [("ra", j), ("tb", b)] + ([("hb", 1 - b)] if j < 15 else [])
                                P.add("dve", lambda e, b=b, sl=sl, init=init: e.tensor_tensor_scan(
                                    out=hb_[b][:, ::-1], data0=ra[:, sl][:, ::-1], data1=tb_[b][:, ::-1], initial=init,
                                    op0=ALU.mult, op1=ALU.add), rd, [("hb", b)])
                                P.at(n_, 3)
                                TT(ty_[b][:], hf[:, sl], hb_[b][:], ALU.add, [("hf", j), ("hb", b)], [("ty", b)], eng="pool")
                                ACT(tq_[b][:], ty_[b][:], AF.Square, [("ty", b)], [("tq", b)])
                                pss = psF[3]
                                c0 = (n_ % 8) * 4
                                ks = ("psF", 3)
                                MMS([(pss[:, c0 + s:c0 + s + 1], [(tq_[b][:, s * 128:(s + 1) * 128], ones[:, 0:1])]) for s in range(4)],
                                    [("tq", b), "ones"], [ks])
                                P.at(n_, 4)
                                STT(tz_[b][:], ty_[b][:], pc("lg", ch), sg[:, sl], ALU.mult, ALU.mult,
                                    [("ty", b), ("sg", cp, j), "prm"], [("tz", b)])
                                P.at(n_, 5)
                                if ch == 0:
                                    CP(ss[:, 1, j * 4:(j + 1) * 4], pss[:, c0:c0 + 4], [ks], [("ss1", j)])
                                else:
                                    TT(ss[:, 1, j * 4:(j + 1) * 4], ss[:, 1, j * 4:(j + 1) * 4], pss[:, c0:c0 + 4], ALU.add,
                                       [ks, ("ss1", j)], [("ss1", j)])
                                DMA("sp", zd[DH + ch * 128:DH + (ch + 1) * 128, sl], tz_[b][:], [("tz", b)], [("zd", 6 + ch, j)])
                        for _ in P.pipe_steps():
                            yield "C"

                def run_merged(gy, gx):
                    cnt = 0
                    cntb = 0

                    def adv():
                        for t in gx:
                            if t != "s":
                                return
                    for tag in gy:
                        if tag == "C":
                            cnt += 1
                            if cnt % 2 == 0:
                                adv()
                        elif tag == "B":
                            cntb += 1
                            adv()
                        elif tag == "A":
                            next(gx, None)
                    for _ in gx:
                        pass

                for _ in X(0):
                    pass
                for ch in range(nch):
                    if ch + 1 < nch:
                        run_merged(Y(ch), X(ch + 1))
                    else:
                        for _ in Y(ch):
                            pass
            P.barrier()

        def ph3():
            with ExitStack() as ph:
                def sbp(name, shape, dt):
                    return ph.enter_context(nc.sbuf_tensor(name, list(shape), dt))
                wst = sbp("wst3", [128, 8, 512], F32)
                wb = sbp("wb3", [128, 8, 512], BF16)
                xt = [sbp("xt3%d" % i, [128, 8, 512], BF16) for i in range(2)]
                raw3 = [sbp("raw3%d" % i, [128, L + 8], BF16) for i in range(3)]
                dg = sbp("dg3", [128, 9, 128], BF16)
                tv_ = [sbp("tv%d" % i, [128, 512], F32) for i in range(2)]
                tc_ = [[sbp("tc%d_%d" % (q, i), [128, 512], F32) for i in range(2)] for q in range(3)]
                tu_ = [sbp("tu%d" % i, [128, 512], BF16) for i in range(2)]
                tx_ = [sbp("tx%d" % i, [128, 512], BF16) for i in range(2)]
                tg_ = [sbp("tg%d" % i, [128, 512], BF16) for i in range(2)]
                for q in range(3):
                    MSET(raw3[q][:, 0:1], 0.0, [], ["rawpad3"])
                    MSET(raw3[q][:, L + 1:L + 8], 0.0, [], ["rawpad3"])
                for ch in range(6):
                    wprep(wst, wb, [ch * 128, DH + ch * 128, 2 * DH + ch * 128, 3 * DH + ch * 128], "h")
                    for j in range(16):
                        b = j % 2
                        sl = slice(j * 512, (j + 1) * 512)
                        DMA("sp", xt[b][:], xnTd[:, :, sl].rearrange("dc p t -> p dc t"), [("xnTd", j)], [("xt3", b)])
                        for q in range(4):
                            pq = psF[q]
                            MM(pq[:, :], [(wb[:, dc, q * 128:(q + 1) * 128], xt[b][:, dc, :]) for dc in range(8)],
                               [("h", "wb"), ("xt3", b)], [("psF", q)])
                            if q < 3:
                                ACT(raw3[q][:, 1 + j * 512:1 + (j + 1) * 512], pq[:, :], AF.Copy, [("psF", q)], [("raw3", q, j)])
                            else:
                                ACT(tg_[b][:], pq[:, :], AF.Silu, [("psF", q)], [("tg", b)])
                                DMA("pool", sgd[ch * 128:(ch + 1) * 128, sl], tg_[b][:], [("tg", b)], [("sgd", ch, j)])
                        for jj in ([j - 1] if j > 0 else []) + ([15] if j == 15 else []):
                            conv3(ACT, STT, TT, DMA, raw3, tc_, tv_, tu_, tx_, pc, ud, x0d, ch, jj)
            P.barrier()

        def ph4a(Mres, F1):
            with ExitStack() as ph:
                def sbp(name, shape, dt):
                    return ph.enter_context(nc.sbuf_tensor(name, list(shape), dt))
                Gres = sbp("Gres", [128, 65, 3, 128], BF16)
                Em = sbp("Em", [128, 64], BF16)
                uTp = [sbp("uTp%d" % i, [64, 8, 128], BF16) for i in range(2)]
                Aa = sbp("Aa4", [128, 130, 128], BF16)
                Bp = sbp("Bp", [128, 128, 128], BF16)
                kg = [sbp("kg4%d" % i, [128, 6, 2, 128], F32) for i in range(2)]
                tq = [sbp("tq4%d" % i, [128, 4, 2, 128], BF16) for i in range(2)]
                yT = [sbp("yT%d" % i, [64, 8, 128], F32) for i in range(3)]
                for h_ in range(5):
                    DMA("sp", Gres[:, h_ * 13:(h_ + 1) * 13, :, :], c_G[:, h_ * 13:(h_ + 1) * 13, :, :], [], ["G"])
                DMA("sp", Em[:], c_E[:, :], [], ["Em"])
                BT = Aa
                NP2 = 33
                for ch in range(6):
                    def dT(c):
                        return uTp[(c // 8) % 2][0:64, c % 8, :]
                    def pre(c, ch=ch):
                        if c % 8 == 0:
                            q = c // 8
                            DMA("sp", uTp[q % 2][:],
                                ud[ch * 128 + q * 8:ch * 128 + (q + 1) * 8, :].rearrange("c (a b) -> a c b", b=128),
                                [], [("uT", q % 2)])
                    fft_s1(MM, ACT, CP, psF, dT, F1, Aa, 64, lambda c: ("uT", (c // 8) % 2), pre=pre)

                    def s3(p):
                        pX = psF[p % 4]
                        nj = 2 if p < 32 else 1
                        for j in range(nj):
                            k1 = 2 * p + j
                            fft_s3(MMS, pX[:, j * 256:(j + 1) * 256], Mres, k1, Aa, k1, ["M"] + AA_KEYS, [("psF", p % 4)])
                    s3(0)
                    s3(1)
                    s3(2)
                    for p in range(NP2):
                        nj = 2 if p < 32 else 1
                        k1a = 2 * p
                        g = k1a // 6
                        kk = k1a % 6
                        b = g % 2
                        if kk == 0:
                            nk = min(6, 65 - 6 * g)
                            DMA("sp", kg[b][:, 0:nk, :, :], Kd[ch, :, g * 6:g * 6 + nk, :, :], [("Kd", ch, g)], [("kg", b)])
                        if p + 3 < NP2:
                            s3(p + 3)
                        qb = p % 2
                        pX = psF[p % 4]
                        kX = ("psF", p % 4)
                        Xv = pX[:, 0:nj * 256].rearrange("p (j r c) -> p j r c", j=nj, r=2)
                        Xr, Xi = Xv[:, :, 0, :], Xv[:, :, 1, :]
                        Kv = kg[b][:, kk:kk + nj, :, :]
                        Kr, Ki = Kv[:, :, 0, :], Kv[:, :, 1, :]
                        tqk = [("tq", qb, i) for i in range(4)]
                        TT(tq[qb][:, 0, 0:nj, :], Xr, Kr, ALU.mult, [kX, ("kg", b)], [tqk[0]])
                        STT(tq[qb][:, 1, 0:nj, :], Xi, -1.0, Ki, ALU.mult, ALU.mult, [kX, ("kg", b)], [tqk[1]])
                        TT(tq[qb][:, 2, 0:nj, :], Xr, Ki, ALU.mult, [kX, ("kg", b)], [tqk[2]])
                        TT(tq[qb][:, 3, 0:nj, :], Xi, Kr, ALU.mult, [kX, ("kg", b)], [tqk[3]])
                        pBq = psF[4 + p % 2]
                        kB = ("psF", 4 + p % 2)
                        grp = []
                        for j in range(nj):
                            k1 = k1a + j
                            G0, G1, G2 = Gres[:, k1, 0, :], Gres[:, k1, 1, :], Gres[:, k1, 2, :]
                            t = [tq[qb][:, i, j, :] for i in range(4)]
                            grp.append((pBq[:, j * 256:j * 256 + 128], [(G0, t[0]), (G0, t[1]), (G2, t[2]), (G2, t[3])]))
                            grp.append((pBq[:, j * 256 + 128:j * 256 + 256], [(G1, t[0]), (G1, t[1]), (G0, t[2]), (G0, t[3])]))
                        MMS(grp, ["G"] + tqk, [kB])
                        Bv = pBq[:, 0:nj * 256].rearrange("p (j r c) -> p j r c", j=nj, r=2)
                        ACT_COPY(Bp[:, k1a:k1a + nj, :], Bv[:, :, 0, :], [kB], [("Bp", k1a), ("Bp", k1a + 1)])
                        js = [j for j in range(nj) if 1 <= k1a + j <= 63]
                        if js:
                            j0, j1 = js[0], js[-1] + 1
                            ACT_COPY(Bp[:, 64 + k1a + j0:64 + k1a + j1, :], Bv[:, j0:j1, 1, :], [kB],
                                     [("Bp", 64 + k1a + j) for j in js])
                    bpk = [("Bp", k) for k in range(129)]
                    for c8 in range(16):
                        b = c8 % 2

                        def trs(e, b=b, c8=c8):
                            ins = None
                            for i in range(8):
                                ins = e.transpose(psB[b][:, i * 128:(i + 1) * 128], Bp[:, :, c8 * 8 + i], ident[:])
                            return ins
                        P.add("pe", trs, bpk + ["ident"], [("psB", b)])
                        (CP if c8 % 2 == 0 else ACT_COPY)(BT[:, c8 * 8:(c8 + 1) * 8, :], psB[b][:, :].rearrange("p (c n) -> p c n", c=8),
                                                         [("psB", b)], [("BT", c8)])
                    for c8 in range(16):
                        b = c8 % 3
                        for i in range(2):
                            c0 = c8 * 8 + i * 4
                            py = psF[(c8 * 2 + i) % 6]
                            ky = ("psF", (c8 * 2 + i) % 6)
                            MM(py[0:64, :], [(Em[:, 0:64], BT[:, c0:c0 + 4, :])], ["Em", ("BT", c8)] + AA_KEYS, [ky])
                            ACT(yT[b][:, i * 4:(i + 1) * 4, :], py[0:64, :].rearrange("p (c n) -> p c n", c=4), AF.Copy,
                                [ky], [("yT", b)])
                        DMA("pool", yd[ch * 128 + c8 * 8:ch * 128 + (c8 + 1) * 8, :].rearrange("c (a b) -> a c b", b=128),
                            yT[b][:], [("yT", b)], [("yd", ch, c8)])
            P.barrier()

        def ph4b():
            with ExitStack() as ph:
                def sbp(name, shape, dt):
                    return ph.enter_context(nc.sbuf_tensor(name, list(shape), dt))
                fy = [sbp("fy%d" % i, [128, 1024], F32) for i in range(2)]
                fu = [sbp("fu%d" % i, [128, 1024], BF16) for i in range(3)]
                fx = [sbp("fx%d" % i, [128, 1024], BF16) for i in range(4)]
                fs = [sbp("fs%d" % i, [128, 1024], BF16) for i in range(6)]
                fa = [sbp("fa%d" % i, [128, 1024], F32) for i in range(2)]
                f2_ = [sbp("f2_%d" % i, [128, 1024], F32) for i in range(2)]
                fb = [sbp("fb%d" % i, [128, 1024], F32) for i in range(3)]
                fq = [sbp("fq%d" % i, [128, 1024], BF16) for i in range(2)]
                fz = [sbp("fz%d" % i, [128, 1024], BF16) for i in range(2)]
                P.pipe_begin()
                for ch in range(6):
                    for j in range(8):
                        T_ = ch * 8 + j
                        b = T_ % 2
                        sl = slice(j * 1024, (j + 1) * 1024)
                        rows = slice(ch * 128, (ch + 1) * 128)
                        b3, b4, b6 = T_ % 3, T_ % 4, T_ % 6
                        P.at(T_, 0)
                        DMA("sp", fy[b][:], yd[rows, sl], [], [("fy", b)])
                        DMA("sp", fu[b3][:], ud[rows, sl], [], [("fu", b3)])
                        DMA("sp", fx[b4][:], x0d[rows, sl], [], [("fx", b4)])
                        DMA("sp", fs[b6][:], sgd[rows, sl], [], [("fs", b6)])
                        P.at(T_, 1)
                        ACT(fa[b][:], fy[b][:], AF.Identity, [("fy", b), "invn"], [("fa", b)], scale=invn[:, ch:ch + 1])
                        P.at(T_, 2)
                        STT(f2_[b][:], fu[b3][:], pc("hskip", ch), fa[b][:], ALU.mult, ALU.add, [("fu", b3), ("fa", b), "prm"], [("f2", b)])
                        P.at(T_, 3)
                        TT(fb[b3][:], f2_[b][:], fx[b4][:], ALU.mult, [("f2", b), ("fx", b4)], [("fb", b3)], eng="pool")
                        P.at(T_, 4)
                        ACT(fq[b][:], fb[b3][:], AF.Square, [("fb", b3)], [("fq", b)])
                        pss = psF[T_ % 6]
                        ks = ("psF", T_ % 6)
                        MMS([(pss[:, s:s + 1], [(fq[b][:, s * 128:(s + 1) * 128], ones[:, 0:1])]) for s in range(8)],
                            [("fq", b), "ones"], [ks])
                        P.at(T_, 5)
                        if ch == 0:
                            CP(ss[:, 0, j * 8:(j + 1) * 8], pss[:, 0:8], [ks], [("ss0", j)])
                        else:
                            TT(ss[:, 0, j * 8:(j + 1) * 8], ss[:, 0, j * 8:(j + 1) * 8], pss[:, 0:8], ALU.add,
                               [ks, ("ss0", j)], [("ss0", j)])
                        STT(fz[b][:], fb[b3][:], pc("hg", ch), fs[b6][:], ALU.mult, ALU.mult, [("fb", b3), ("fs", b6), "prm"], [("fz", b)])
                        P.at(T_, 6)
                        DMA("pool", zd[rows, sl], fz[b][:], [("fz", b)], [("zd", ch, 2 * j), ("zd", ch, 2 * j + 1)])
                P.pipe_end()
            P.barrier()

        def ph5():
            with ExitStack() as ph:
                def sbp(name, shape, dt):
                    return ph.enter_context(nc.sbuf_tensor(name, list(shape), dt))
                wos = sbp("wos", [128, DM], F32)
                wo = sbp("wo", [128, 12, DM], BF16)
                fgb = sbp("fgb", [128, DM], F32)
                rs = sbp("rs", [128, 2, 64], F32)
                zt = [sbp("zt5%d" % i, [128, 12, 512], BF16) for i in range(2)]
                xr = [sbp("xr%d" % i, [128, DM], F32) for i in range(3)]
                o1 = [sbp("o1%d" % i, [128, DM], F32) for i in range(2)]
                o2 = [sbp("o2%d" % i, [128, DM], F32) for i in range(2)]
                o3 = [sbp("o3%d" % i, [128, DM], F32) for i in range(2)]
                ojk = sbp("ojk", [128, DM], F32)
                s5 = sbp("s5", [128, 64], F32)
                r5 = sbp("r5", [128, 64], F32)
                DMA("sp", fgb[:], fg_d[:, :], [], ["fgb"])
                for cc in range(12):
                    DMA("sp", wos[:], w_out[cc * 128:(cc + 1) * 128, :], [], ["wos"])
                    CP(wo[:, cc, :], wos[:], ["wos"], ["wo"])
                ssk = [("ss0", j) for j in range(8)] + [("ss1", j) for j in range(16)]
                TS(rs[:], ss[:], 1.0 / DH, EPS, ALU.mult, ALU.add, ssk, ["rs"])
                ACT(rs[:], rs[:], AF.Sqrt, ["rs"], ["rs"])
                P.add("dve", lambda e: e.reciprocal(out=rs[:], in_=rs[:]), ["rs"], ["rs"])
                P.pipe_begin()
                for g in range(16):
                    gb = g % 2
                    zk = [("zt5", gb, c0) for c0 in (0, 3, 6, 9)]
                    for s in range(4):
                        tt = g * 4 + s
                        b = tt % 2
                        P.at(tt, 0)
                        if s == 0:
                            for g2 in ([0, 1] if g == 0 else [g + 1]):
                                if g2 < 16:
                                    for br in range(2):
                                        for half in range(2):
                                            c0 = br * 6 + half * 3
                                            DMA("sp", zt[g2 % 2][:, c0:c0 + 3, :],
                                                zd[c0 * 128:(c0 + 3) * 128, g2 * 512:(g2 + 1) * 512].rearrange("(cc p) t -> p cc t", p=128),
                                                [], [("zt5", g2 % 2, c0)])
                        DMA("sp", xr[tt % 3][:], x[tt * 128:(tt + 1) * 128, :], [], [("xr", tt % 3)])
                        tsl = slice(s * 128, (s + 1) * 128)
                        for hh_ in range(2):
                            pp = (2 * tt + hh_) % 3
                            pa, pbk = psF[2 * pp], psF[2 * pp + 1]
                            ka, kb_ = ("psF", 2 * pp), ("psF", 2 * pp + 1)
                            esl = slice(hh_ * 512, (hh_ + 1) * 512)
                            MM(pa[:, :], [(zt[gb][:, cc, tsl], wo[:, cc, esl]) for cc in range(6)], zk + ["wo"], [ka])
                            MM(pbk[:, :], [(zt[gb][:, 6 + cc, tsl], wo[:, 6 + cc, esl]) for cc in range(6)], zk + ["wo"], [kb_])
                            ACT(o1[b][:, esl], pa[:, :], AF.Identity, [ka, "rs"], [("o1", b, hh_)], scale=rs[:, 0, tt:tt + 1])
                            STT(o1[b][:, esl], pbk[:, :], rs[:, 1, tt:tt + 1], o1[b][:, esl], ALU.mult, ALU.add,
                                [kb_, ("o1", b, hh_), "rs"], [("o1", b, hh_)])
                        P.at(tt, 1)
                        TT(o2[b][:], o1[b][:], xr[tt % 3][:], ALU.add, [("o1", b, 0), ("o1", b, 1), ("xr", tt % 3)], [("o2", b)], eng="pool")
                        ACT(ojk[:], o2[b][:], AF.Square, [("o2", b)], ["ojk", ("s5", tt)], accum=s5[:, tt:tt + 1])
                        P.at(tt, 2)
                        TS(r5[:, tt:tt + 1], s5[:, tt:tt + 1], 1.0 / DM, EPS, ALU.mult, ALU.add, [("s5", tt)], [("r5", tt)])
                        ACT(r5[:, tt:tt + 1], r5[:, tt:tt + 1], AF.Sqrt, [("r5", tt)], [("r5", tt)])
                        P.add("dve", lambda e, tt=tt: e.reciprocal(out=r5[:, tt:tt + 1], in_=r5[:, tt:tt + 1]), [("r5", tt)], [("r5", tt)])
                        STT(o3[b][:], o2[b][:], r5[:, tt:tt + 1], fgb[:], ALU.mult, ALU.mult,
                            [("o2", b), ("r5", tt), "fgb"], [("o3", b)])
                        DMA("pool", out_d[tt * 128:(tt + 1) * 128, :], o3[b][:], [("o3", b)], [("out", tt)])
                P.pipe_end()
            P.barrier()

        if 1 in phases:
            ph1()
        if 2 in phases:
            ph2()
        if 3 in phases:
            ph3()
        if 0 in phases:
            ph0ab()
        if 0 in phases or 4 in phases:
            with ExitStack() as phm:
                Mres = phm.enter_context(nc.sbuf_tensor("Mres", [128, 65, 3, 128], BF16))
                F1 = phm.enter_context(nc.sbuf_tensor("F1", [128, 130], BF16))
                for h_ in range(5):
                    DMA("sp", Mres[:, h_ * 13:(h_ + 1) * 13, :, :], c_M[:, h_ * 13:(h_ + 1) * 13, :, :], [], ["M"])
                DMA("sp", F1[:], c_F1[:, :], [], ["F1"])
                if 0 in phases:
                    ph0c(Mres, F1)
                if 4 in phases:
                    ph4a(Mres, F1)
        if 4 in phases:
            ph4b()
        if debug:
            DMA("pool", ssd.rearrange("a p t -> p a t"), ss[:], [], ["ssd"])
            P.barrier()
        if 5 in phases:
            ph5()

        P.emit(nc, es)
    return nc


NGRP = 43
AA_KEYS = [("Aa", g) for g in range(NGRP)]


def fft_s1(MM, ACT, CP, psF, dT, F1, Aa, K, key, pre=None):
    c = 0
    g = 0
    while c < 128:
        n = min(3, 128 - c)
        pst = psF[g % 6]
        pk = ("psF", g % 6)
        for i in range(n):
            if pre is not None:
                pre(c + i)
            MM(pst[:, i * 130:(i + 1) * 130], [(dT(c + i), F1[0:K, 0:130])], [key(c + i), "F1"], [pk])
        o = Aa[:, :, c:c + n].rearrange("p k c -> p c k")
        i_ = pst[:, 0:n * 130].rearrange("p (c k) -> p c k", c=n)
        if g % 2 == 0:
            ACT(o, i_, mybir.ActivationFunctionType.Copy, [pk], [("Aa", g)])
        else:
            CP(o, i_, [pk], [("Aa", g)])
        c += n
        g += 1


def fft_s3(MMS, pX, Mres, mk, Aa, k1, R, W):
    Mr, Mi, nMi = Mres[:, mk, 0, :], Mres[:, mk, 1, :], Mres[:, mk, 2, :]
    Ar, Ai = Aa[:, k1, :], Aa[:, 65 + k1, :]
    MMS([(pX[:, 0:128], [(Mr, Ar), (nMi, Ai)]), (pX[:, 128:256], [(Mi, Ar), (Mr, Ai)])], R, W)


def conv3(ACT, STT, TT, DMA, raw3, tc_, tv_, tu_, tx_, pc, ud, x0d, ch, j):
    b = j % 2
    sl = slice(j * 512, (j + 1) * 512)
    rows = slice(ch * 128, (ch + 1) * 128)
    AFT = mybir.ActivationFunctionType
    MUL, ADD = mybir.AluOpType.mult, mybir.AluOpType.add

    def rd(q):
        return [("raw3", q, j), "rawpad3", "prm"] + ([("raw3", q, j + 1)] if j < 15 else []) + ([("raw3", q, j - 1)] if j > 0 else [])

    def w(q, k):
        return pc("hcw", k * 18 + q * 6 + ch)

    def tap(q, k):
        return raw3[q][:, j * 512 + k:j * 512 + k + 512]
    outs = [tv_[b], tx_[b], tc_[2][b]]
    okeys = [("tv", b), ("tx", b), ("tc", 2, b)]
    for q in range(3):
        t = tc_[q][b]
        tk = ("tc", q, b)
        ACT(t[:], tap(q, 1), AFT.Identity, rd(q), [tk], bias=pc("hcb", q * 6 + ch), scale=w(q, 1))
        STT(t[:], tap(q, 0), w(q, 0), t[:], MUL, ADD, rd(q) + [tk], [tk])
        STT(outs[q][:], tap(q, 2), w(q, 2), t[:], MUL, ADD, rd(q) + [tk], [okeys[q]])
        if q == 1:
            DMA("pool", x0d[rows, sl], tx_[b][:], [("tx", b)], [("x0d", ch, j)])
    TT(tu_[b][:], tc_[2][b][:], tv_[b][:], MUL, [("tc", 2, b), ("tv", b)], [("tu", b)])
    DMA("pool", ud[rows, sl], tu_[b][:], [("tu", b)], [("ud", ch, j)])


_NC = {}


def _in_map(inp, b):
    c = _constants()
    m = dict(c)
    m["x"] = np.ascontiguousarray(inp["x"][b])
    m["w_in"] = np.ascontiguousarray(inp["w_in"][0])
    m["w_out"] = np.ascontiguousarray(inp["w_out"][0])
    m["prm"] = _pack_params(inp)
    m["bd"] = _blockdiag(inp)
    m["fg"] = np.ascontiguousarray(np.broadcast_to(inp["final_g"][None, :], (128, DM))).astype(np.float32)
    m["flt_w1"] = np.ascontiguousarray(inp["flt_w1"][0])
    m["flt_w2"] = np.ascontiguousarray(inp["flt_w2"][0])
    m["flt_w3"] = np.ascontiguousarray(inp["flt_w3"][0])
    m["flt_w4"] = np.ascontiguousarray(inp["flt_w4"][0])
    return m


def kernel(**inputs):
    inp = {k: np.asarray(v) for k, v in inputs.items()}
    if "nc" not in _NC:
        _NC["nc"] = build()
    nc = _NC["nc"]
    in_maps = [_in_map(inp, b) for b in range(8)]
    res = run_bass_kernel_spmd(nc, in_maps, core_ids=list(range(8)))
    return np.stack([np.asarray(r["out"]).reshape(L, DM) for r in res.results], axis=0).astype(np.float32)
```
